# Optimizing a Trainium2 kernel written in Bass

```python
import math
import jax, jax.numpy as jnp
from jax import lax
import numpy as np

D_MODEL = 1024
BATCH = 1
SEQ = 16384
DEPTH = 1
DEC_BATCH = 16
DEC_SEQ = 4096
PAST_LEN = 128

D_MIX = D_MODEL
D_FOURIER = D_MIX // 2
D_SSM = D_MIX - D_FOURIER
N_FOURIER_HEADS = 4
FOURIER_HEAD_DIM = D_FOURIER // N_FOURIER_HEADS
SSM_GROUP = 16
N_SSM_GROUPS = D_SSM // SSM_GROUP
SSM_STATE = 64
N_DIR = 2
D_FF = 4 * D_MODEL
N_MOD = 6
EPS = 1e-6
DT_MIN = 1e-3
DT_MAX = 1e-1
A_RE_MAX = -1e-4

kernel_name = "hymba_fnet_s5_adaln_encoder"


def rmsnorm(x, g):
    xf = x.astype(jnp.float32)
    y = xf * lax.rsqrt(jnp.mean(xf * xf, axis=-1, keepdims=True) + EPS)
    return (y * g.astype(jnp.float32)).astype(x.dtype)


def fourier_mixer(u, w_f, b_f):
    b, l, _ = u.shape
    z = jnp.fft.fft2(u.astype(jnp.float32), axes=(1, 2), norm="ortho").real.astype(u.dtype)
    z = z.reshape(b, l, N_FOURIER_HEADS, FOURIER_HEAD_DIM)
    y = jnp.einsum("blhd,hde->blhe", z, w_f).reshape(b, l, D_FOURIER)
    return y + b_f


def _scan_combine(left, right):
    a_l, b_l = left
    a_r, b_r = right
    return a_r * a_l, a_r * b_l + b_r


def s5_direction(u_c, a_re, a_im, log_dt, b_re, b_im, c_re, c_im, reverse):
    f32 = jnp.float32
    lam = lax.complex(jnp.minimum(a_re.astype(f32), A_RE_MAX), a_im.astype(f32))
    dt = jnp.exp(log_dt.astype(f32))[:, None]
    a_bar = jnp.exp(lam * dt)
    b_mat = lax.complex(b_re.astype(f32), b_im.astype(f32))
    b_bar = ((a_bar - 1.0) / lam)[..., None] * b_mat
    bu = jnp.einsum("blgh,gph->blgp", u_c, b_bar)
    a_seq = jnp.broadcast_to(a_bar, bu.shape)
    _, h = lax.associative_scan(_scan_combine, (a_seq, bu), axis=1, reverse=reverse)
    c_mat = lax.complex(c_re.astype(f32), c_im.astype(f32))
    return jnp.einsum("blgp,ghp->blgh", h, c_mat).real


def s5_mixer(u, a_re, a_im, log_dt, b_re, b_im, c_re, c_im, d_skip, w_glu, b_glu):
    b, l, _ = u.shape
    uf = u.astype(jnp.float32).reshape(b, l, N_SSM_GROUPS, SSM_GROUP)
    u_c = lax.complex(uf, jnp.zeros_like(uf))
    y_fwd = s5_direction(u_c, a_re[0], a_im[0], log_dt[0], b_re[0], b_im[0], c_re[0], c_im[0], False)
    y_bwd = s5_direction(u_c, a_re[1], a_im[1], log_dt[1], b_re[1], b_im[1], c_re[1], c_im[1], True)
    y = (y_fwd + y_bwd).reshape(b, l, D_SSM) + d_skip.astype(jnp.float32) * uf.reshape(b, l, D_SSM)
    g = jax.nn.gelu(y).astype(u.dtype)
    ab = g @ w_glu + b_glu
    val, gate = jnp.split(ab, 2, axis=-1)
    return val * jax.nn.sigmoid(gate)


def encoder_layer(x, c, w_ada, b_ada, g_mix_norm, w_in, w_fourier, b_fourier,
                  ssm_a_re, ssm_a_im, ssm_log_dt, ssm_b_re, ssm_b_im, ssm_c_re, ssm_c_im,
                  ssm_d, w_glu, b_glu, g_fourier_out, g_ssm_out, w_out,
                  g_mlp_norm, w_mlp_in, b_mlp_in, w_mlp_out, b_mlp_out):
    mod = (jax.nn.silu(c) @ w_ada + b_ada)[:, None, :]
    shift1, scale1, gate1, shift2, scale2, gate2 = jnp.split(mod, N_MOD, axis=-1)
    h = rmsnorm(x, g_mix_norm) * (1.0 + scale1) + shift1
    z = h @ w_in
    y_f = fourier_mixer(z[..., :D_FOURIER], w_fourier, b_fourier)
    y_s = s5_mixer(z[..., D_FOURIER:], ssm_a_re, ssm_a_im, ssm_log_dt, ssm_b_re, ssm_b_im,
                   ssm_c_re, ssm_c_im, ssm_d, w_glu, b_glu)
    merged = jnp.concatenate([rmsnorm(y_f, g_fourier_out), rmsnorm(y_s, g_ssm_out)], axis=-1)
    x = x + gate1 * (merged @ w_out)
    h = rmsnorm(x, g_mlp_norm) * (1.0 + scale2) + shift2
    ff = jnp.square(jax.nn.relu(h @ w_mlp_in + b_mlp_in)) @ w_mlp_out + b_mlp_out
    return x + gate2 * ff


def setup_inputs(seed: int = 0) -> dict:
    key = jax.random.key(seed)
    ks = jax.random.split(key, 32)
    f32 = jnp.float32
    nrm = lambda k, shape, s: (jax.random.normal(k, shape, f32) * s).astype(f32)
    G, P, H = N_SSM_GROUPS, SSM_STATE, SSM_GROUP
    a_im_base = jnp.pi * jnp.arange(P, dtype=f32)
    return {
        "x_prompt": nrm(ks[0], (BATCH, SEQ, D_MODEL), 1.0),
        "x_sample": nrm(ks[1], (DEC_BATCH, DEC_SEQ, D_MODEL), 1.0),
        "c_prompt": nrm(ks[2], (BATCH, D_MODEL), 1.0),
        "c_sample": nrm(ks[3], (DEC_BATCH, D_MODEL), 1.0),
        "w_ada": nrm(ks[4], (DEPTH, D_MODEL, N_MOD * D_MODEL), 0.5 * D_MODEL ** -0.5),
        "b_ada": nrm(ks[5], (DEPTH, N_MOD * D_MODEL), 0.02),
        "g_mix_norm": 1.0 + nrm(ks[6], (DEPTH, D_MODEL), 0.02),
        "w_in": nrm(ks[7], (DEPTH, D_MODEL, D_MIX), D_MODEL ** -0.5),
        "w_fourier": nrm(ks[8], (DEPTH, N_FOURIER_HEADS, FOURIER_HEAD_DIM, FOURIER_HEAD_DIM), FOURIER_HEAD_DIM ** -0.5),
        "b_fourier": nrm(ks[9], (DEPTH, D_FOURIER), 0.02),
        "ssm_a_re": -0.5 + nrm(ks[10], (DEPTH, N_DIR, G, P), 0.01),
        "ssm_a_im": a_im_base + nrm(ks[11], (DEPTH, N_DIR, G, P), 0.01),
        "ssm_log_dt": jax.random.uniform(ks[12], (DEPTH, N_DIR, G), f32, math.log(DT_MIN), math.log(DT_MAX)),
        "ssm_b_re": nrm(ks[13], (DEPTH, N_DIR, G, P, H), (2 * H) ** -0.5),
        "ssm_b_im": nrm(ks[14], (DEPTH, N_DIR, G, P, H), (2 * H) ** -0.5),
        "ssm_c_re": nrm(ks[15], (DEPTH, N_DIR, G, H, P), P ** -0.5),
        "ssm_c_im": nrm(ks[16], (DEPTH, N_DIR, G, H, P), P ** -0.5),
        "ssm_d": nrm(ks[17], (DEPTH, D_SSM), 0.5),
        "w_glu": nrm(ks[18], (DEPTH, D_SSM, 2 * D_SSM), D_SSM ** -0.5),
        "b_glu": nrm(ks[19], (DEPTH, 2 * D_SSM), 0.02),
        "g_fourier_out": 1.0 + nrm(ks[20], (DEPTH, D_FOURIER), 0.02),
        "g_ssm_out": 1.0 + nrm(ks[21], (DEPTH, D_SSM), 0.02),
        "w_out": nrm(ks[22], (DEPTH, D_MIX, D_MODEL), D_MIX ** -0.5),
        "g_mlp_norm": 1.0 + nrm(ks[23], (DEPTH, D_MODEL), 0.02),
        "w_mlp_in": nrm(ks[24], (DEPTH, D_MODEL, D_FF), D_MODEL ** -0.5),
        "b_mlp_in": nrm(ks[25], (DEPTH, D_FF), 0.02),
        "w_mlp_out": nrm(ks[26], (DEPTH, D_FF, D_MODEL), D_FF ** -0.5),
        "b_mlp_out": nrm(ks[27], (DEPTH, D_MODEL), 0.02),
        "g_final": 1.0 + nrm(ks[28], (D_MODEL,), 0.02),
    }


def reference(x_prompt, x_sample, c_prompt, c_sample, w_ada, b_ada, g_mix_norm, w_in,
              w_fourier, b_fourier, ssm_a_re, ssm_a_im, ssm_log_dt, ssm_b_re, ssm_b_im,
              ssm_c_re, ssm_c_im, ssm_d, w_glu, b_glu, g_fourier_out, g_ssm_out, w_out,
              g_mlp_norm, w_mlp_in, b_mlp_in, w_mlp_out, b_mlp_out, g_final):
    def trunk(x, c):
        for i in range(DEPTH):
            x = encoder_layer(x, c, w_ada[i], b_ada[i], g_mix_norm[i], w_in[i], w_fourier[i],
                              b_fourier[i], ssm_a_re[i], ssm_a_im[i], ssm_log_dt[i], ssm_b_re[i],
                              ssm_b_im[i], ssm_c_re[i], ssm_c_im[i], ssm_d[i], w_glu[i], b_glu[i],
                              g_fourier_out[i], g_ssm_out[i], w_out[i], g_mlp_norm[i],
                              w_mlp_in[i], b_mlp_in[i], w_mlp_out[i], b_mlp_out[i])
        return rmsnorm(x, g_final)

    y_prompt = trunk(x_prompt, c_prompt)
    y_sample = trunk(x_sample, c_sample)
    return (y_prompt, y_sample)
```

```python
import math
from contextlib import ExitStack
import numpy as np
import ml_dtypes
import concourse.bass as bass
import concourse.mybir as mybir
from concourse.bass_utils import run_bass_kernel_spmd

F32 = mybir.dt.float32
BF16 = mybir.dt.bfloat16
ALU = mybir.AluOpType
AF = mybir.ActivationFunctionType
EPS = 1e-6
JB = 128
JB2 = 512


class _Eng:
    def __init__(self, name, eng, sem):
        self.name, self.eng, self.sem = name, eng, sem
        self.count = 0
        self.waited = {}


class Buf:
    __slots__ = ("name", "w", "r")

    def __init__(self, name=""):
        self.name = name
        self.w = None
        self.r = []


class Sched:
    def __init__(self, nc, stack, n_dma_slots=10, same_engine_sync=True):
        self.nc = nc
        self.E = {}
        for name, eng in (("pe", nc.tensor), ("act", nc.scalar), ("dve", nc.vector),
                          ("pool", nc.gpsimd), ("sp", nc.sync)):
            sem = stack.enter_context(nc.semaphore("s_" + name))
            self.E[name] = _Eng(name, eng, sem)
        self.same = same_engine_sync
        self.slots = {}
        for q in ("sp", "act", "pool"):
            sl = []
            for i in range(n_dma_slots):
                sem = stack.enter_context(nc.semaphore("d_%s%d" % (q, i)))
                sl.append([sem, 0])
            self.slots[q] = [sl, 0]
        self.n_instr = 0

    def _wait(self, e, sem, val):
        key = id(sem)
        if e.waited.get(key, 0) >= val:
            return
        e.eng.wait_ge(sem, val)
        e.waited[key] = val

    def _need(self, e, ev):
        if ev[2] == e.name:
            if e.name == "pe":
                return False
            if isinstance(self.same, (set, frozenset, tuple, list)):
                return e.name in self.same
            return self.same
        return True

    def _deps(self, e, reads, writes):
        for b in reads:
            if b.w is not None and self._need(e, b.w):
                self._wait(e, b.w[0], b.w[1])
        for b in writes:
            if b.w is not None and self._need(e, b.w):
                self._wait(e, b.w[0], b.w[1])
            for ev in b.r:
                if self._need(e, ev):
                    self._wait(e, ev[0], ev[1])

    def op(self, engname, fn, reads=(), writes=()):
        e = self.E[engname]
        self._deps(e, reads, writes)
        ins = fn(e.eng)
        e.count += 1
        ins.then_inc(e.sem, 1)
        ev = (e.sem, e.count, e.name)
        for b in reads:
            b.r.append(ev)
            if len(b.r) > 24:
                b.r = b.r[-24:] if False else b.r
        for b in writes:
            b.w = ev
            b.r = []
        self.n_instr += 1
        return ins

    def dma(self, q, out, in_, reads=(), writes=(), **kw):
        e = self.E[q]
        self._deps(e, reads, writes)
        sl, idx = self.slots[q]
        slot = sl[idx % len(sl)]
        self.slots[q][1] = idx + 1
        if slot[1] > 0:
            self._wait(e, slot[0], slot[1])
        ins = e.eng.dma_start(out=out, in_=in_, **kw)
        slot[1] += 16
        ins.then_inc(slot[0], 16)
        ev = (slot[0], slot[1], "dma_" + q)
        for b in reads:
            b.r.append(ev)
        for b in writes:
            b.w = ev
            b.r = []
        self.n_instr += 1
        return ins

    def barrier(self):
        names = ("pe", "act", "dve", "pool", "sp")
        for n in names:
            e = self.E[n]
            for q in self.slots:
                for slot in self.slots[q][0]:
                    if slot[1] > 0:
                        self._wait(e, slot[0], slot[1])
            for o in names:
                if o != n and self.E[o].count > 0:
                    self._wait(e, self.E[o].sem, self.E[o].count)

    def finish(self):
        e = self.E["sp"]
        for q in self.slots:
            for slot in self.slots[q][0]:
                if slot[1] > 0:
                    self._wait(e, slot[0], slot[1])
        for n in ("pe", "act", "dve", "pool"):
            o = self.E[n]
            if o.count > 0:
                self._wait(e, o.sem, o.count)


class Rot:
    def __init__(self, items):
        self.items = items
        self.i = 0

    def next(self):
        it = self.items[self.i % len(self.items)]
        self.i += 1
        return it


VEC_LAYOUT = {}


def _vec_layout(nseq):
    off = 0
    lay = {}
    for name, w in (("b_ada", 48), ("g_mix", 8), ("g_mlp", 8), ("g_final", 8), ("b_glu", 8),
                    ("b_mlp_out", 8), ("b_mlp_in", 32), ("g_f", 4), ("g_s", 4), ("b_f", 4), ("ssm_d", 4),
                    ("m0H", 1), ("m1H", 1), ("m0S", 1), ("m1S", 1), ("nm0S", 1), ("nm1S", 1),
                    ("cT", 8 * nseq)):
        lay[name] = (off, w)
        off += w
    return lay, off


def _pk(v):
    v = np.asarray(v, np.float32).reshape(-1, 128)
    return np.ascontiguousarray(v.T)


def _dft_consts(L, rot_i=None, Lout=None):
    N1 = L // 128
    n1 = np.arange(N1)
    ang = 2 * np.pi * ((n1[:, None] * n1[None, :]) % N1) / N1
    Fc, Fs = np.cos(ang), np.sin(ang)
    FF1 = np.concatenate([Fc, -Fs], axis=1)
    FF2 = np.concatenate([-Fs, -Fc], axis=1)
    n2 = np.arange(128, dtype=np.int64)
    if rot_i is None:
        shift, k0, Lo = 0, 0, L
    else:
        shift, k0, Lo = Lout * rot_i, Lout * rot_i, Lout
    k = k0 + np.arange(Lo, dtype=np.int64)
    angw = 2 * np.pi * (((n2[:, None] + shift) * k[None, :]) % L) / L
    sc = 1.0 / math.sqrt(512.0 * L)
    f = lambda a: np.ascontiguousarray(a, dtype=np.float32).astype(ml_dtypes.bfloat16)
    return dict(FF1=f(FF1), FF2=f(FF2), Wc=f(np.cos(angw) * sc), Ws=f(np.sin(angw) * sc))


def _host_common(inp, nseq):
    lay, nv = _vec_layout(nseq)
    vecs = np.zeros((128, nv), np.float32)

    def put(name, arr):
        o, w = lay[name]
        vecs[:, o:o + w] = arr

    put("b_ada", _pk(inp["b_ada"][0]))
    put("g_mix", _pk(inp["g_mix_norm"][0]))
    put("g_mlp", _pk(inp["g_mlp_norm"][0]))
    put("g_final", _pk(inp["g_final"]))
    put("b_glu", _pk(inp["b_glu"][0]))
    put("b_mlp_out", _pk(inp["b_mlp_out"][0]))
    put("b_mlp_in", _pk(inp["b_mlp_in"][0]))
    put("g_f", _pk(inp["g_fourier_out"][0]))
    put("g_s", _pk(inp["g_ssm_out"][0]))
    put("b_f", _pk(inp["b_fourier"][0]))
    put("ssm_d", _pk(inp["ssm_d"][0]))
    p = np.arange(128)
    m0H = (((p // 16) % 2) == 0).astype(np.float32)
    m0S = (p < 64).astype(np.float32)
    put("m0H", m0H[:, None]); put("m1H", (1 - m0H)[:, None])
    put("m0S", m0S[:, None]); put("m1S", (1 - m0S)[:, None])
    put("nm0S", -m0S[:, None]); put("nm1S", -(1 - m0S)[:, None])
    c = np.arange(512)
    angc = 2 * np.pi * ((c[:, None] * c[None, :]) % 512) / 512
    com = dict(
        w_ada=np.ascontiguousarray(inp["w_ada"][0]), w_in=np.ascontiguousarray(inp["w_in"][0]),
        w_glu=np.ascontiguousarray(inp["w_glu"][0]), w_out=np.ascontiguousarray(inp["w_out"][0]),
        w_mlp_in=np.ascontiguousarray(inp["w_mlp_in"][0]), w_mlp_out=np.ascontiguousarray(inp["w_mlp_out"][0]),
        w_f=np.ascontiguousarray(inp["w_fourier"][0]),
        CC=np.cos(angc).astype(np.float32), SC=np.sin(angc).astype(np.float32),
        ident=np.eye(128, dtype=np.float32),
    )
    a_re, a_im, ldt = inp["ssm_a_re"][0], inp["ssm_a_im"][0], inp["ssm_log_dt"][0]
    b_re, b_im, c_re, c_im = inp["ssm_b_re"][0], inp["ssm_b_im"][0], inp["ssm_c_re"][0], inp["ssm_c_im"][0]

    def hmaj_a(a):
        t = a.reshape(2, 4, 8, 64)
        t = np.broadcast_to(t[:, :, :, None, :], (2, 4, 8, 16, 64))
        return np.ascontiguousarray(t.transpose(2, 3, 0, 1, 4).reshape(128, 2, 4, 64), dtype=np.float32)

    def hmaj_b(b):
        t = b.reshape(2, 4, 8, 64, 16)
        return np.ascontiguousarray(t.transpose(2, 4, 0, 1, 3).reshape(128, 2, 4, 64), dtype=np.float32)

    def smaj_a(a):
        t = a.reshape(2, 4, 4, 2, 64)
        return np.ascontiguousarray(t.transpose(3, 4, 0, 1, 2).reshape(128, 2, 4, 4), dtype=np.float32)

    def smaj_c(cc_):
        t = cc_.reshape(2, 4, 4, 2, 16, 64)
        return np.ascontiguousarray(t.transpose(3, 5, 0, 1, 2, 4).reshape(128, 2, 4, 4, 16), dtype=np.float32)

    def smaj_b(b):
        t = b.reshape(2, 4, 4, 2, 64, 16)
        return np.ascontiguousarray(t.transpose(3, 4, 0, 1, 2, 5).reshape(128, 2, 4, 4, 16), dtype=np.float32)

    ldt3 = np.broadcast_to(ldt[:, :, None], (2, 32, 64))
    com.update(
        aH=np.stack([hmaj_a(a_re), hmaj_a(a_im), hmaj_a(ldt3)], axis=1),
        bH=np.stack([hmaj_b(b_re), hmaj_b(b_im)], axis=1),
        aS=np.stack([smaj_a(a_re), smaj_a(a_im), smaj_a(ldt3)], axis=1),
        cS=np.stack([smaj_c(c_re), smaj_c(c_im)], axis=1),
        bS=np.stack([smaj_b(b_re), smaj_b(b_im)], axis=1),
    )
    return com, vecs, lay


def build(seqs, dbg=False, same=('pool', 'dve'), skip=''):
    seqs_full = [(a, a) if isinstance(a, int) else tuple(a) for a in seqs]
    louts = [b for a, b in seqs_full]
    seqs = [a for a, b in seqs_full]
    nseq = len(seqs)
    lay, nv = _vec_layout(nseq)
    nc = bass.Bass("TRN2", target_bir_lowering=False)
    DI = lambda name, shape, dt=F32: nc.dram_tensor(name, list(shape), dt, kind="ExternalInput").ap()
    DO = lambda name, shape, dt=F32: nc.dram_tensor(name, list(shape), dt, kind="ExternalOutput").ap()
    DS = lambda name, shape, dt=F32: nc.dram_tensor(name, list(shape), dt, kind=("ExternalOutput" if dbg else "Internal")).ap()

    xT = [DI("xT%d" % q, [1024, L]) for q, L in enumerate(seqs)]
    yT = [DO("yT%d" % q, [1024, louts[q]]) for q, L in enumerate(seqs)]
    vecs_d = DI("vecs", [128, nv])
    w_ada_d = DI("w_ada", [1024, 6144]); w_in_d = DI("w_in", [1024, 1024]); w_glu_d = DI("w_glu", [512, 1024])
    w_out_d = DI("w_out", [1024, 1024]); w_mi_d = DI("w_mlp_in", [1024, 4096]); w_mo_d = DI("w_mlp_out", [4096, 1024])
    w_f_d = DI("w_f", [4, 128, 128]); CC_d = DI("CC", [512, 512]); SC_d = DI("SC", [512, 512]); ident_d = DI("ident", [128, 128])
    aH_d = DI("aH", [128, 3, 2, 4, 64]); bH_d = DI("bH", [128, 2, 2, 4, 64]); aS_d = DI("aS", [128, 3, 2, 4, 4])
    cS_d = DI("cS", [128, 2, 2, 4, 4, 16]); bS_d = DI("bS", [128, 2, 2, 4, 4, 16])
    dftc = {}
    mskd = {}
    for q, L in enumerate(seqs):
        N1 = L // 128
        nk2 = louts[q] // N1
        dftc[q] = dict(FF1=DI("FF1_q%d" % q, [N1, 2 * N1], BF16), FF2=DI("FF2_q%d" % q, [N1, 2 * N1], BF16),
                       Wc=DI("Wc_q%d" % q, [128, louts[q]], BF16), Ws=DI("Ws_q%d" % q, [128, louts[q]], BF16))
        if louts[q] < L:
            mskd[q] = DI("msk_q%d" % q, [128, 2, L // 8], BF16)
    AB = [DS("AB%d" % q, [4, L, 256], BF16) for q, L in enumerate(seqs)]
    US = [DS("US%d" % q, [512, L], BF16) for q, L in enumerate(seqs)]
    YFd = [DS("YF%d" % q, [512, louts[q]], BF16) for q, L in enumerate(seqs)]
    GS = [DS("GS%d" % q, [512, louts[q]], BF16) for q, L in enumerate(seqs)]
    X1 = [DS("X1%d" % q, [1024, louts[q]], F32) for q, L in enumerate(seqs)]
    wb_in = DS("wb_in", [1024, 1024], BF16); wb_glu = DS("wb_glu", [512, 1024], BF16); wb_out = DS("wb_out", [1024, 1024], BF16)
    wb_mi = DS("wb_mi", [1024, 4096], BF16); wb_mo = DS("wb_mo", [4096, 1024], BF16)
    INd = DS("INd", [128, 4, 2, 8, 2, 128], BF16)
    OUTd = DS("OUTd", [128, 4, 2, 4, 8, 2, 32], BF16)
    KMd = DS("KMd", [128, 4, 15, 128], BF16)
    TBd = DS("TBd", [128, 4, 8, 2, JB2], BF16)
    MODd = DS("MODd", [128, 48, nseq], F32) if dbg else None
    Gd = DS("Gd", [128, 4, 1024], BF16) if dbg else None
    Zd = DS("Zd", [128, 8, 512], BF16) if dbg else None
    ABd = DS("ABd", [128, 4, 1024], BF16) if dbg else None

    with ExitStack() as top:
        S = Sched(nc, top, same_engine_sync=same)
        _cnt = [0]

        def SB(st, name, shape, dt=F32):
            _cnt[0] += 1
            return st.enter_context(nc.sbuf_tensor("sb%d_%s" % (_cnt[0], name), list(shape), dt))
        PS = Rot([(top.enter_context(nc.psum_tensor("ps%d" % i, [128, 512], F32)), Buf("ps%d" % i)) for i in range(8)])

        def mm(out, lhsT, rhs, start, stop, reads, writes, **kw):
            return S.op("pe", lambda e: e.matmul(out, lhsT=lhsT, rhs=rhs, start=start, stop=stop, **kw), reads, writes)

        def act(out, in_, func, reads, writes, **kw):
            return S.op("act", lambda e: e.activation(out=out, in_=in_, func=func, **kw), reads, writes)

        def tt(eng, out, in0, in1, op, reads, writes):
            return S.op(eng, lambda e: e.tensor_tensor(out=out, in0=in0, in1=in1, op=op), reads, writes)

        def ts(eng, out, in0, s1, s2, op0, op1, reads, writes):
            if s2 is None:
                return S.op(eng, lambda e: e.tensor_scalar(out=out, in0=in0, scalar1=s1, scalar2=None, op0=op0), reads, writes)
            return S.op(eng, lambda e: e.tensor_scalar(out=out, in0=in0, scalar1=s1, scalar2=s2, op0=op0, op1=op1), reads, writes)

        def stt(eng, out, in0, scalar, in1, op0, op1, reads, writes):
            return S.op(eng, lambda e: e.scalar_tensor_tensor(out=out, in0=in0, scalar=scalar, in1=in1, op0=op0, op1=op1), reads, writes)

        def cp(eng, out, in_, reads, writes):
            if eng == "act":
                return act(out, in_, AF.Identity, reads, writes)
            return S.op(eng, lambda e: e.tensor_copy(out=out, in_=in_), reads, writes)

        vecs = SB(top, "vecs", [128, nv]); Bvecs = Buf("vecs")
        S.dma("sp", vecs[:], vecs_d[:, :], writes=[Bvecs])
        V = lambda name: vecs[:, lay[name][0]:lay[name][0] + lay[name][1]]
        Bwb = dict(w_in=Buf(), w_glu=Buf(), w_out=Buf(), w_mi=Buf(), w_mo=Buf())
        S.dma("pool", wb_in[:, :], w_in_d[:, :], writes=[Bwb["w_in"]])
        ones_bf = SB(top, "ones_bf", [128, 128], BF16); Bones = Buf("ones")
        S.op("pool", lambda e: e.memset(ones_bf[:], 1.0), writes=[Bones])
        ident = SB(top, "ident", [128, 128]); Bident = Buf("ident")
        S.dma("sp", ident[:], ident_d[:, :], writes=[Bident])
        mod = SB(top, "mod", [128, 48, nseq]); Bmod = Buf("mod")
        A1 = SB(top, "A1", [128, 8, nseq]); A2 = SB(top, "A2", [128, 8, nseq]); bg2 = SB(top, "bg2", [128, 8, nseq])
        zB = SB(top, "zB", [128, 8, nseq]); mB = SB(top, "mB", [128, 32, nseq])
        sh = SB(top, "sh_bf", [128, 16, nseq], BF16); Bsh = Buf("sh")
        BA1, BA2, Bbg2, BzB, BmB = Buf("A1"), Buf("A2"), Buf("bg2"), Buf("zB"), Buf("mB")
        r_sc = SB(top, "r_sc", [128, 32]); EJ = SB(top, "EJ", [128, 2, 32]); E1 = SB(top, "E1", [128, 2, 32]); Bsc_c = Buf("scanconst")
        stG = ExitStack()
        Gcat = SB(stG, "Gcat", [128, 4, 1024], BF16); BG = Buf("Gcat")

        stbox = [None]
        if True:
            def cmul(o_re, o_im, a_re, a_im, b_re, b_im, t1, t2, Bs):
                tt("dve", t1, a_re, b_re, ALU.mult, Bs, Bs); tt("dve", t2, a_im, b_im, ALU.mult, Bs, Bs)
                tt("dve", o_re, t1, t2, ALU.subtract, Bs, Bs)
                tt("dve", t1, a_re, b_im, ALU.mult, Bs, Bs); tt("dve", t2, a_im, b_re, ALU.mult, Bs, Bs)
                tt("dve", o_im, t1, t2, ALU.add, Bs, Bs)

            def cx_setup(name, a3, F, Bs):
                W = SB(stbox[0], name + "_w", [128, 12, F])
                PW = SB(stbox[0], name + "_pw", [128, 9, 2, F])
                lam, dt, x16, th, m, sn, cs_, t1, t2, cr, ci, den = [W[:, i, :] for i in range(12)]
                ts("dve", lam, a3[:, 0, :], -1e-4, None, ALU.min, None, Bs, Bs)
                act(dt, a3[:, 2, :], AF.Exp, Bs, Bs)
                tt("dve", x16, lam, dt, ALU.mult, Bs, Bs); ts("dve", x16, x16, 1.0 / 16, None, ALU.mult, None, Bs, Bs)
                tt("dve", th, a3[:, 1, :], dt, ALU.mult, Bs, Bs); ts("dve", th, th, 1.0 / 16, None, ALU.mult, None, Bs, Bs)
                ts("dve", m, x16, 1.0 / 6, 0.5, ALU.mult, ALU.add, Bs, Bs)
                tt("dve", m, m, x16, ALU.mult, Bs, Bs); ts("dve", m, m, 1.0, None, ALU.add, None, Bs, Bs)
                tt("dve", m, m, x16, ALU.mult, Bs, Bs); ts("dve", m, m, 1.0, None, ALU.add, None, Bs, Bs)
                act(sn, th, AF.Sin, Bs, Bs)
                act(cs_, th, AF.Sin, Bs, Bs, bias=math.pi / 2)
                are, aim = PW[:, 1, 0, :], PW[:, 1, 1, :]
                tt("dve", are, m, cs_, ALU.mult, Bs, Bs); tt("dve", aim, m, sn, ALU.mult, Bs, Bs)
                for _ in range(4):
                    tt("dve", t1, are, are, ALU.mult, Bs, Bs); tt("dve", t2, aim, aim, ALU.mult, Bs, Bs)
                    tt("dve", aim, are, aim, ALU.mult, Bs, Bs); ts("dve", aim, aim, 2.0, None, ALU.mult, None, Bs, Bs)
                    tt("dve", are, t1, t2, ALU.subtract, Bs, Bs)
                S.op("dve", lambda e: e.memset(PW[:, 0, 0, :], 1.0), Bs, Bs)
                S.op("dve", lambda e: e.memset(PW[:, 0, 1, :], 0.0), Bs, Bs)
                for k in range(2, 9):
                    cmul(PW[:, k, 0, :], PW[:, k, 1, :], PW[:, k - 1, 0, :], PW[:, k - 1, 1, :], are, aim, t1, t2, Bs)
                nr = x16
                ts("dve", nr, are, -1.0, None, ALU.add, None, Bs, Bs)
                tt("dve", t1, lam, lam, ALU.mult, Bs, Bs); tt("dve", t2, a3[:, 1, :], a3[:, 1, :], ALU.mult, Bs, Bs)
                tt("dve", den, t1, t2, ALU.add, Bs, Bs)
                S.op("dve", lambda e: e.reciprocal(out=den, in_=den), Bs, Bs)
                tt("dve", t1, nr, lam, ALU.mult, Bs, Bs); tt("dve", t2, aim, a3[:, 1, :], ALU.mult, Bs, Bs)
                tt("dve", cr, t1, t2, ALU.add, Bs, Bs); tt("dve", cr, cr, den, ALU.mult, Bs, Bs)
                tt("dve", t1, aim, lam, ALU.mult, Bs, Bs); tt("dve", t2, nr, a3[:, 1, :], ALU.mult, Bs, Bs)
                tt("dve", ci, t1, t2, ALU.subtract, Bs, Bs); tt("dve", ci, ci, den, ALU.mult, Bs, Bs)
                return PW, cr, ci, t1, t2

        stH = ExitStack()
        st = stH
        stbox[0] = st
        BH = [Buf("ssmH")]
        aH = SB(st, "aH", [128, 3, 512]); bH = SB(st, "bH", [128, 2, 512])
        S.dma("sp", aH[:], aH_d.rearrange("p a d c s -> p a (d c s)"), writes=BH)
        S.dma("sp", bH[:], bH_d.rearrange("p a d c s -> p a (d c s)"), writes=BH)
        PWH, crH, ciH, t1H, t2H = cx_setup("H", aH, 512, BH)
        bbH = SB(st, "bbH", [128, 2, 512])
        cmul(bbH[:, 0, :], bbH[:, 1, :], crH, ciH, bH[:, 0, :], bH[:, 1, :], t1H, t2H, BH)
        INm = SB(st, "INm", [128, 4, 2, 8, 2, 128], BF16)
        tmpH = SB(st, "tmpH", [128, 2, 256])
        for d in range(2):
            dsl = slice(d * 256, (d + 1) * 256)
            for s_ in range(8):
                e_ = 7 - s_ if d == 0 else s_
                cmul(tmpH[:, 0, :], tmpH[:, 1, :], PWH[:, e_, 0, dsl], PWH[:, e_, 1, dsl], bbH[:, 0, dsl], bbH[:, 1, dsl],
                     t1H[:, 0:256], t2H[:, 0:256], BH)
                for ri in range(2):
                    src = tmpH[:, ri, :].rearrange("p (c s) -> p c s", s=64)
                    ts("dve", INm[:, :, d, s_, ri, 0:64], src, V("m0H"), None, ALU.mult, None, BH + [Bvecs], BH)
                    ts("dve", INm[:, :, d, s_, ri, 64:128], src, V("m1H"), None, ALU.mult, None, BH + [Bvecs], BH)
        S.dma("sp", INd.rearrange("p c d s r k -> p (c d s r k)"), INm[:].rearrange("p c d s r k -> p (c d s r k)"), reads=BH)

        with ExitStack() as st:
            sc = SB(st, "silu_c", [128, 8 * nseq]); Bsc = Buf()
            act(sc[:], V("cT"), AF.Silu, [Bvecs], [Bsc])
            scv = sc[:].rearrange("p (k q) -> p k q", q=nseq)
            wa = [(SB(st, "wa%d" % i, [128, 8, 1024]), Buf()) for i in range(2)]
            for cb in range(6):
                wt, Bw = wa[cb % 2]
                S.dma("sp", wt[:], w_ada_d.rearrange("(k p) n -> p k n", p=128)[:, :, cb * 1024:(cb + 1) * 1024], writes=[Bw])
                for m in range(8):
                    ps, Bp = PS.next()
                    for k in range(8):
                        mm(ps[:, 0:nseq], wt[:, k, m * 128:(m + 1) * 128], scv[:, k, :], k == 0, k == 7, [Bw, Bsc], [Bp])
                    j = cb * 8 + m
                    act(mod[:, j, :], ps[:, 0:nseq], AF.Identity, [Bp, Bvecs], [Bmod], bias=V("b_ada")[:, j:j + 1])
            if dbg:
                S.dma("sp", MODd[:, :, :], mod[:], reads=[Bmod])
            ts("dve", A1[:], mod[:, 8:16, :], 1.0, None, ALU.add, None, [Bmod], [BA1])
            tt("dve", A1[:], A1[:], V("g_mix").unsqueeze(2).to_broadcast([128, 8, nseq]), ALU.mult, [BA1, Bvecs], [BA1])
            ts("dve", A2[:], mod[:, 32:40, :], 1.0, None, ALU.add, None, [Bmod], [BA2])
            tt("dve", A2[:], A2[:], V("g_mlp").unsqueeze(2).to_broadcast([128, 8, nseq]), ALU.mult, [BA2, Bvecs], [BA2])
            tt("dve", bg2[:], mod[:, 40:48, :], V("b_mlp_out").unsqueeze(2).to_broadcast([128, 8, nseq]), ALU.mult, [Bmod, Bvecs], [Bbg2])
            cp("dve", sh[:, 0:8, :], mod[:, 0:8, :], [Bmod], [Bsh])
            cp("dve", sh[:, 8:16, :], mod[:, 24:32, :], [Bmod], [Bsh])
            cs = SB(st, "cs", [128, 2, 4, 512], BF16); Bcs = Buf()
            S.dma("pool", cs[:, 0, :, :], CC_d.rearrange("(h k) c -> k h c", k=128), writes=[Bcs])
            S.dma("pool", cs[:, 1, :, :], SC_d.rearrange("(h k) c -> k h c", k=128), writes=[Bcs])
            wf = SB(st, "wf", [128, 4, 128], BF16); Bwf = Buf()
            S.dma("pool", wf[:], w_f_d.rearrange("h d e -> d h e"), writes=[Bwf])
            for h in range(4):
                for ri in range(2):
                    ps, Bp = PS.next()
                    for kc in range(4):
                        mm(ps[:, kc * 128:(kc + 1) * 128], cs[:, ri, h, kc * 128:(kc + 1) * 128], wf[:, h, :], True, True, [Bcs, Bwf], [Bp])
                    cp("act", Gcat[:, :, h * 256 + ri * 128: h * 256 + ri * 128 + 128],
                       ps[:, :].rearrange("p (k e) -> p k e", e=128), [Bp], [BG])

        if dbg:
            S.dma("sp", Gd[:, :, :], Gcat[:], reads=[BG])
        S.barrier()
        stH.close()
        with ExitStack() as st:
            w_in = SB(st, "w_in", [128, 8, 1024], BF16); Bwin = Buf()
            S.dma("sp", w_in[:], wb_in.rearrange("(k p) n -> p k n", p=128), reads=[Bwb["w_in"]], writes=[Bwin])
            for m in range(8):
                ps, Bp = PS.next()
                for k in range(8):
                    mm(ps[:, 0:nseq], w_in[:, k, m * 128:(m + 1) * 128], sh[:, k, :], k == 0, k == 7, [Bwin, Bsh], [Bp])
                cp("act", zB[:, m, :], ps[:, 0:nseq], [Bp], [BzB])
            xin = Rot([(SB(st, "xin%d" % i, [128, 8, 512]), Buf()) for i in range(2)])
            sqr = Rot([(SB(st, "sq%d" % i, [128, 8, 512], BF16), Buf()) for i in range(2)])
            rsr = Rot([(SB(st, "rs%d" % i, [128, 512]), Buf()) for i in range(2)])
            hbr = Rot([(SB(st, "hb%d" % i, [128, 8, 512], BF16), Buf()) for i in range(2)])
            zt = Rot([(SB(st, "zt%d" % i, [128, 8, 512], BF16), Buf()) for i in range(2)])
            abt = Rot([(SB(st, "abt%d" % i, [128, 4, 1024], BF16), Buf()) for i in range(2)])
            tiles = [] if 'A' in skip else [(q, ti) for q, L in enumerate(seqs) for ti in range(L // 512)]
            stA = {}

            def a_load(i):
                q, ti = tiles[i]
                xt, Bx = xin.next()
                S.dma("sp", xt[:], xT[q].rearrange("(k p) t -> p k t", p=128)[:, :, ti * 512:(ti + 1) * 512], writes=[Bx])
                stA[i] = dict(xt=xt, Bx=Bx)

            def a_square(i):
                d_ = stA[i]
                sq, Bsq = sqr.next()
                act(sq[:], d_["xt"][:], AF.Square, [d_["Bx"]], [Bsq])
                d_.update(sq=sq, Bsq=Bsq)

            def a_norm(i):
                q, ti = tiles[i]
                d_ = stA[i]
                rs, Brs = rsr.next(); hb, Bh = hbr.next()
                ps, Bp = PS.next()
                for k in range(8):
                    mm(ps[:, :], ones_bf[:], d_["sq"][:, k, :], k == 0, k == 7, [Bones, d_["Bsq"]], [Bp])
                act(rs[:], ps[:, :], AF.Sqrt, [Bp], [Brs], bias=EPS, scale=1.0 / 1024)
                S.op("dve", lambda e: e.reciprocal(out=rs[:], in_=rs[:]), [Brs], [Brs])
                for k in range(8):
                    stt("dve", hb[:, k, :], d_["xt"][:, k, :], A1[:, k, q:q + 1], rs[:], ALU.mult, ALU.mult, [d_["Bx"], BA1, Brs], [Bh])
                d_.update(hb=hb, Bh=Bh)

            def a_main1(i, half):
                q, ti = tiles[i]
                d_ = stA[i]
                if half == 0:
                    z, Bz = zt.next()
                    d_.update(z=z, Bz=Bz)
                z, Bz = d_["z"], d_["Bz"]
                for m in range(half * 4, half * 4 + 4):
                    ps, Bp = PS.next()
                    for k in range(8):
                        mm(ps[:, :], w_in[:, k, m * 128:(m + 1) * 128], d_["hb"][:, k, :], k == 0, k == 7, [Bwin, d_["Bh"]], [Bp])
                    act(z[:, m, :], ps[:, :], AF.Identity, [Bp, BzB], [Bz], bias=zB[:, m, q:q + 1])
                if half == 1:
                    S.dma("sp", US[q].rearrange("(k p) t -> p k t", p=128)[:, :, ti * 512:(ti + 1) * 512], z[:, 4:8, :], reads=[Bz])

            def a_main2(i):
                q, ti = tiles[i]
                d_ = stA.pop(i)
                z, Bz = d_["z"], d_["Bz"]
                t0 = ti * 512
                ab, Bab = abt.next()
                for j in range(4):
                    for half in range(2):
                        ps, Bp = PS.next()
                        for kc in range(4):
                            mm(ps[:, :], z[:, kc, j * 128:(j + 1) * 128], Gcat[:, kc, half * 512:(half + 1) * 512],
                               kc == 0, kc == 3, [Bz, BG], [Bp])
                        cp("dve" if half == 0 else "act", ab[:, j, half * 512:(half + 1) * 512], ps[:, :], [Bp], [Bab])
                for j in range(4):
                    S.dma("sp", AB[q][:, t0 + j * 128:t0 + (j + 1) * 128, :].rearrange("g p c -> p g c"),
                          ab[:, j, :].rearrange("p (g c) -> p g c", c=256), reads=[Bab])

            nA = len(tiles)
            if nA:
                a_load(0); a_square(0); a_norm(0)
                if nA > 1:
                    a_load(1)
            for i in range(nA):
                if i + 1 < nA:
                    a_square(i + 1)
                a_main1(i, 0)
                if i + 1 < nA:
                    a_norm(i + 1)
                a_main1(i, 1)
                if i + 2 < nA:
                    a_load(i + 2)
                a_main2(i)

        S.barrier()
        stG.close()
        S.dma("pool", wb_glu[:, :], w_glu_d[:, :], writes=[Bwb["w_glu"]])
        S.dma("pool", wb_out[:, :], w_out_d[:, :], writes=[Bwb["w_out"]])
        S.dma("pool", wb_mi[:, :], w_mi_d[:, :], writes=[Bwb["w_mi"]])
        S.dma("pool", wb_mo[:, :], w_mo_d[:, :], writes=[Bwb["w_mo"]])
        for q, L in enumerate(seqs):
            if 'F' in skip:
                break
            N1 = L // 128
            S.barrier()
            with ExitStack() as st:
                dc = dftc[q]
                Lo = louts[q]; nk2 = Lo // N1; kpb = min(512 // nk2, N1)
                FF1 = SB(st, "FF1", [N1, 2 * N1], BF16); FF2 = SB(st, "FF2", [N1, 2 * N1], BF16)
                Wc = SB(st, "Wc", [128, Lo], BF16); Ws = SB(st, "Ws", [128, Lo], BF16)
                Bc = Buf("dftc")
                S.dma("sp", FF1[:], dc["FF1"][:, :], writes=[Bc]); S.dma("sp", FF2[:], dc["FF2"][:, :], writes=[Bc])
                S.dma("sp", Wc[:], dc["Wc"][:, :], writes=[Bc]); S.dma("sp", Ws[:], dc["Ws"][:, :], writes=[Bc])
                X = SB(st, "X", [N1, 128, 256], BF16); BX = Buf("X")
                Ypr = Rot([(SB(st, "Yp%d" % i, [128, 128, 2, N1], BF16), Buf("Yp")) for i in range(1 if N1 == 128 else 2)])
                YFr = Rot([(SB(st, "YF%d" % i, [128, Lo], BF16), Buf("YF")) for i in range(1 if N1 == 128 else 2)])
                nb = 512 // (2 * N1)
                S.dma("sp", X[:], AB[q][0].rearrange("(a b) c -> a b c", b=128), reads=[], writes=[BX])
                for g4 in range(4):
                    Yp, BYp = Ypr.next(); YF, BYF = YFr.next()
                    for bi in range(128 // nb):
                        ps, Bp = PS.next()
                        for ci in range(nb):
                            c = bi * nb + ci
                            mm(ps[:, ci * 2 * N1:(ci + 1) * 2 * N1], X[:, :, c], FF1[:], True, False, [BX, Bc], [Bp])
                            mm(ps[:, ci * 2 * N1:(ci + 1) * 2 * N1], X[:, :, 128 + c], FF2[:], False, True, [BX, Bc], [Bp])
                        cp("act" if bi % 2 == 0 else "dve", Yp[:, bi * nb:(bi + 1) * nb, :, :],
                           ps[:, :].rearrange("p (c r k) -> p c r k", r=2, k=N1), [Bp], [BYp])
                    if g4 + 1 < 4:
                        S.dma("sp", X[:], AB[q][g4 + 1].rearrange("(a b) c -> a b c", b=128), reads=[], writes=[BX])
                    YFv = YF[:].rearrange("c (k2 k1) -> c k1 k2", k1=N1)
                    Wcv = Wc[:].rearrange("p (k2 k1) -> p k1 k2", k1=N1); Wsv = Ws[:].rearrange("p (k2 k1) -> p k1 k2", k1=N1)
                    for kb in range(N1 // kpb):
                        ps, Bp = PS.next()
                        for i in range(kpb):
                            k1 = kb * kpb + i
                            mm(ps[:, i * nk2:(i + 1) * nk2], Yp[:, :, 0, k1], Wcv[:, k1, :], True, False, [BYp, Bc], [Bp])
                            mm(ps[:, i * nk2:(i + 1) * nk2], Yp[:, :, 1, k1], Wsv[:, k1, :], False, True, [BYp, Bc], [Bp])
                        act(YFv[:, kb * kpb:(kb + 1) * kpb, :], ps[:, 0:kpb * nk2].rearrange("p (i k) -> p i k", k=nk2), AF.Identity,
                            [Bp, Bvecs], [BYF], bias=V("b_f")[:, g4:g4 + 1])
                    S.dma("sp", YFd[q][g4 * 128:(g4 + 1) * 128, :], YF[:], reads=[BYF])

        S.barrier()

        S.barrier()
        with ExitStack() as st:
            stbox[0] = st
            BS_ = [Buf("ssmS")]
            aS = SB(st, "aS", [128, 3, 32]); cS = SB(st, "cS", [128, 2, 32, 16]); bS = SB(st, "bS", [128, 2, 32, 16])
            S.dma("sp", aS[:], aS_d.rearrange("p a d c s -> p a (d c s)"), writes=BS_)
            S.dma("sp", cS[:], cS_d.rearrange("p a d c s h -> p a (d c s) h"), writes=BS_)
            S.dma("sp", bS[:], bS_d.rearrange("p a d c s h -> p a (d c s) h"), writes=BS_)
            PWS, crS, ciS, t1S, t2S = cx_setup("S", aS, 32, BS_)
            big = SB(st, "bigS", [128, 2, 32, 16])
            bb16 = lambda ap: ap.unsqueeze(2).to_broadcast([128, 32, 16])
            bbS = SB(st, "bbS", [128, 2, 32, 16])
            cmul(bbS[:, 0], bbS[:, 1], bb16(crS), bb16(ciS), bS[:, 0], bS[:, 1], big[:, 0], big[:, 1], BS_)
            caS = SB(st, "caS", [128, 9, 2, 32, 16])
            for e_ in range(9):
                cmul(caS[:, e_, 0], caS[:, e_, 1], bb16(PWS[:, e_, 0, :]), bb16(PWS[:, e_, 1, :]), cS[:, 0], cS[:, 1], big[:, 0], big[:, 1], BS_)
            OUT9 = SB(st, "OUT9", [128, 4, 2, 4, 9, 2, 32], BF16)
            Bm = SB(st, "Bm", [128, 32, 2, 32], BF16)
            for ri in range(2):
                ts("dve", Bm[:, :, ri, 0:16], bbS[:, ri], V("m0S"), None, ALU.mult, None, BS_ + [Bvecs], BS_)
                ts("dve", Bm[:, :, ri, 16:32], bbS[:, ri], V("m1S"), None, ALU.mult, None, BS_ + [Bvecs], BS_)
                for d in range(2):
                    for e_ in range(9):
                        ei = e_ if d == 0 else 8 - e_
                        src = caS[:, e_, ri, d * 16:(d + 1) * 16, :].rearrange("p (c s) h -> p c s h", s=4)
                        ts("dve", OUT9[:, :, d, :, ei, ri, 0:16], src, V("m0S") if ri == 0 else V("nm0S"), None, ALU.mult, None, BS_ + [Bvecs], BS_)
                        ts("dve", OUT9[:, :, d, :, ei, ri, 16:32], src, V("m1S") if ri == 0 else V("nm1S"), None, ALU.mult, None, BS_ + [Bvecs], BS_)
            for cc in range(4):
                S.dma("sp", OUTd[:, cc, 0].rearrange("p s t r k -> p s (t r k)"),
                      OUT9[:, cc, 0, :, 1:9].rearrange("p s t r k -> p s (t r k)"), reads=BS_)
                S.dma("sp", OUTd[:, cc, 1].rearrange("p s t r k -> p s (t r k)"),
                      OUT9[:, cc, 1, :, 0:8].rearrange("p s t r k -> p s (t r k)"), reads=BS_)
            a8r, a8i = PWS[:, 8, 0, :], PWS[:, 8, 1, :]
            tt("dve", t1S, a8r, a8r, ALU.mult, BS_, BS_); tt("dve", t2S, a8i, a8i, ALU.mult, BS_, BS_)
            tt("dve", t1S, t1S, t2S, ALU.add, BS_, BS_)
            act(r_sc[:], t1S, AF.Sqrt, BS_, [Bsc_c])
            S.op("dve", lambda e: e.reciprocal(out=t1S, in_=r_sc[:]), [Bsc_c] + BS_, BS_)
            TB = SB(st, "TB", [128, 32, 2, JB])
            Ep = SB(st, "Ep", [128, 2, 32]); Ep2 = SB(st, "Ep2", [128, 2, 32])
            tt("dve", Ep[:, 0, :], a8r, t1S, ALU.mult, BS_, BS_); tt("dve", Ep[:, 1, :], a8i, t1S, ALU.mult, BS_, BS_)
            cp("dve", E1[:], Ep[:], BS_, [Bsc_c])
            S.op("dve", lambda e: e.memset(TB[:, :, 0, 0:1], 1.0), BS_, BS_)
            S.op("dve", lambda e: e.memset(TB[:, :, 1, 0:1], 0.0), BS_, BS_)
            tbt = SB(st, "tbt", [128, 2, 32, JB // 2])
            mlen = 1
            while mlen < JB:
                eb = lambda ap: ap.unsqueeze(2).to_broadcast([128, 32, mlen])
                cmul(TB[:, :, 0, mlen:2 * mlen], TB[:, :, 1, mlen:2 * mlen], TB[:, :, 0, 0:mlen], TB[:, :, 1, 0:mlen],
                     eb(Ep[:, 0, :]), eb(Ep[:, 1, :]), tbt[:, 0, :, 0:mlen], tbt[:, 1, :, 0:mlen], BS_)
                tt("dve", Ep2[:, 0, :], Ep[:, 0, :], Ep[:, 0, :], ALU.mult, BS_, BS_); tt("dve", Ep2[:, 1, :], Ep[:, 1, :], Ep[:, 1, :], ALU.mult, BS_, BS_)
                tt("dve", Ep[:, 1, :], Ep[:, 0, :], Ep[:, 1, :], ALU.mult, BS_, BS_); ts("dve", Ep[:, 1, :], Ep[:, 1, :], 2.0, None, ALU.mult, None, BS_, BS_)
                tt("dve", Ep[:, 0, :], Ep2[:, 0, :], Ep2[:, 1, :], ALU.subtract, BS_, BS_)
                mlen *= 2
            Ep3 = SB(st, "Ep3", [128, 2, 32])
            cp("dve", Ep2[:], Ep[:], BS_, BS_)
            tt("dve", Ep3[:, 0, :], Ep[:, 0, :], Ep[:, 0, :], ALU.mult, BS_, BS_); tt("dve", t2S, Ep[:, 1, :], Ep[:, 1, :], ALU.mult, BS_, BS_)
            tt("dve", Ep3[:, 0, :], Ep3[:, 0, :], t2S, ALU.subtract, BS_, BS_)
            tt("dve", Ep3[:, 1, :], Ep[:, 0, :], Ep[:, 1, :], ALU.mult, BS_, BS_); ts("dve", Ep3[:, 1, :], Ep3[:, 1, :], 2.0, None, ALU.mult, None, BS_, BS_)
            tt("dve", EJ[:, 0, :], Ep3[:, 0, :], Ep3[:, 0, :], ALU.mult, BS_, [Bsc_c]); tt("dve", t2S, Ep3[:, 1, :], Ep3[:, 1, :], ALU.mult, BS_, BS_)
            tt("dve", EJ[:, 0, :], EJ[:, 0, :], t2S, ALU.subtract, BS_ + [Bsc_c], [Bsc_c])
            tt("dve", EJ[:, 1, :], Ep3[:, 0, :], Ep3[:, 1, :], ALU.mult, BS_, [Bsc_c]); ts("dve", EJ[:, 1, :], EJ[:, 1, :], 2.0, None, ALU.mult, None, [Bsc_c], [Bsc_c])
            TBx = SB(st, "TBx", [128, 4, 2, JB2]); tbx = SB(st, "tbx_t", [128, 2, 4, JB2 // 2])
            for d in range(2):
                for cc in range(4):
                    f0 = d * 16 + cc * 4
                    cp("dve", TBx[:, :, :, 0:JB], TB[:, f0:f0 + 4], BS_, BS_)
                    for (m0, Epw) in ((JB, Ep2), (2 * JB, Ep3)):
                        eb = lambda ap, m0=m0: ap.unsqueeze(2).to_broadcast([128, 4, m0])
                        cmul(TBx[:, :, 0, m0:2 * m0], TBx[:, :, 1, m0:2 * m0], TBx[:, :, 0, 0:m0], TBx[:, :, 1, 0:m0],
                             eb(Epw[:, 0, f0:f0 + 4]), eb(Epw[:, 1, f0:f0 + 4]), tbx[:, 0, :, 0:m0], tbx[:, 1, :, 0:m0], BS_)
                    S.dma("pool", TBd[:, cc, d * 4:(d + 1) * 4].rearrange("p s r j -> p (s r j)"),
                          TBx[:].rearrange("p s r j -> p (s r j)"), reads=BS_, writes=BS_)
            zer = SB(st, "zer", [128, 128], BF16); Bzer = Buf()
            S.op("pool", lambda e: e.memset(zer[:], 0.0), [], [Bzer])
            KM = SB(st, "KM", [128, 4, 15, 128], BF16); BKM = Buf()
            ktmp = SB(st, "ktmp", [128, 128]); Bkt = Buf()
            for cc in range(4):
                for dl in range(-7, 8):
                    ps, Bp = PS.next()
                    mm(ps[:, 0:128], zer[:], zer[:], True, False, [Bzer], [Bp])
                    dirs = ([0] if dl >= 0 else []) + ([1] if dl <= 0 else [])
                    n_mm = len(dirs) * 8; i_mm = 0
                    for d in dirs:
                        for pi in range(4):
                            Fi = d * 16 + cc * 4 + pi
                            for ri in range(2):
                                i_mm += 1
                                mm(ps[32 * pi:32 * pi + 32, 32 * pi:32 * pi + 32], Bm[:, Fi, ri, :], OUT9[:, cc, d, pi, abs(dl) if d == 0 else 8 - abs(dl), ri, :],
                                   False, i_mm == n_mm, BS_ + [Bzer], [Bp], tile_position=(0, 32 * pi))
                    if dl != 0:
                        cp("act", KM[:, cc, dl + 7, :], ps[:, 0:128], [Bp], [BKM])
                    else:
                        cp("act", ktmp[:], ps[:, 0:128], [Bp], [Bkt])
                        stt("dve", KM[:, cc, 7, :], ident[:], V("ssm_d")[:, cc:cc + 1], ktmp[:], ALU.mult, ALU.add, [Bident, Bvecs, Bkt], [BKM])
            S.dma("sp", KMd.rearrange("p c t k -> p (c t k)"), KM[:].rearrange("p c t k -> p (c t k)"), reads=[BKM])
        S.barrier()

        def bs_group(qs, cc):
            S.barrier()
            with ExitStack() as st:
                INm = SB(st, "INm", [128, 2, 8, 2, 128], BF16); OUTm = SB(st, "OUTm", [128, 2, 4, 8, 2, 32], BF16)
                KMt = SB(st, "KMt", [128, 15, 128], BF16); Bcst = Buf("bs_const")
                S.dma("sp", INm[:].rearrange("p d s r k -> p (d s r k)"), INd[:, cc].rearrange("p d s r k -> p (d s r k)"), writes=[Bcst])
                S.dma("sp", OUTm[:].rearrange("p d s t r k -> p (d s t r k)"), OUTd[:, cc].rearrange("p d s t r k -> p (d s t r k)"), writes=[Bcst])
                S.dma("sp", KMt[:].rearrange("p t k -> p (t k)"), KMd[:, cc].rearrange("p t k -> p (t k)"), writes=[Bcst])
                TBb = SB(st, "TBb", [128, 8, 2, JB2], BF16); BTB = Buf()
                S.dma("sp", TBb[:].rearrange("p s r j -> p (s r j)"), TBd[:, cc].rearrange("p s r j -> p (s r j)"), writes=[BTB])
                NBUF = 4
                Sbr = Rot([(SB(st, "Sb%d" % i, [128, 2, 512], BF16), Buf()) for i in range(NBUF)])
                twr = Rot([(SB(st, "tw%d" % i, [128, 2, 2, 512], BF16), Buf()) for i in range(NBUF)])
                Gtr = Rot([(SB(st, "Gt%d" % i, [128, 2, 512], BF16), Buf()) for i in range(NBUF)])
                Gsr = Rot([(SB(st, "Gs%d" % i, [128, 2, 512], BF16), Buf()) for i in range(NBUF)])
                inir = Rot([(SB(st, "ini%d" % i, [128, 4]), Buf()) for i in range(NBUF)])
                sgn = SB(st, "sgn", [128, 2]); Bsgn = Buf()
                S.op("pool", lambda e: e.memset(sgn[:, 0:1], -1.0), [], [Bsgn])
                S.op("pool", lambda e: e.memset(sgn[:, 1:2], 1.0), [], [Bsgn])
                sgn2 = SB(st, "sgn2", [128, 2])
                S.op("pool", lambda e: e.memset(sgn2[:, 0:1], 1.0), [], [Bsgn])
                S.op("pool", lambda e: e.memset(sgn2[:, 1:2], -1.0), [], [Bsgn])
                ctxs = []
                for q in qs:
                    L = seqs[q]; NC = L // 8; Lo = louts[q]; rot = Lo < L
                    c = dict(q=q, L=L, NC=NC, nblk=NC // 512, Lo=Lo, rot=rot, nco=min(512, Lo // 8))
                    c["nblk_o"] = (Lo // 8) // c["nco"]
                    Ur = SB(st, "Ur", [128, L], BF16); BUr = Buf("Ur")
                    S.dma("sp", Ur[:], US[q][cc * 128:(cc + 1) * 128, :], writes=[BUr])
                    U = SB(st, "U8", [128, 8, NC], BF16); BU = Buf("U8")
                    Urv = Ur[:].rearrange("p (j s) -> p s j", s=8)
                    cp("act", U[:, 0:4, :], Urv[:, 0:4, :], [BUr], [BU])
                    cp("dve", U[:, 4:8, :], Urv[:, 4:8, :], [BUr], [BU])
                    NCs = 512 if rot else NC
                    Hb = SB(st, "Hb", [128, 8, 2, NCs + 2], BF16); BHb = [Buf("Hb%d" % i) for i in range(8)]
                    S.op("pool", lambda e, Hb=Hb: e.memset(Hb[:, :, :, 0:1], 0.0), [], BHb)
                    S.op("pool", lambda e, Hb=Hb, NCs=NCs: e.memset(Hb[:, :, :, NCs + 1:NCs + 2], 0.0), [], BHb)
                    if rot:
                        Mskb = SB(st, "Mskb", [128, 2, NC], BF16); BMsk = Buf()
                        S.dma("sp", Mskb[:], mskd[q][:, :, :], writes=[BMsk])
                        c.update(Mskb=Mskb, BMsk=BMsk)
                    Yst = Rot([(SB(st, "Yst%d" % i, [128, c["nco"] * 8], BF16), Buf()) for i in range(2)])
                    c.update(U=U, BU=BU, Hb=Hb, BHb=BHb, Yst=Yst)
                    ctxs.append(c)

                def dir_gen(c, d):
                    rot, nblk, U, BU, Hb, BHb = c["rot"], c["nblk"], c["U"], c["BU"], c["Hb"], c["BHb"]
                    carry = [None] * 4
                    blocks = [(b_, rot) for b_ in range(nblk)] if d == 0 else [(b_, rot) for b_ in range(nblk - 1, -1, -1)]
                    if rot and d == 0:
                        blocks.append((0, False))
                    for bidx, (b, masked) in enumerate(blocks):
                        stored = (not rot) or (b == 0 and (d == 1 or bidx == nblk))
                        banks = [(PS.next(), PS.next()) for _ in range(4)]
                        for ri in range(2):
                            for s_ in range(8):
                                for pi in range(4):
                                    ps, Bp = banks[pi][ri]
                                    mm(ps[:, :], INm[32 * pi:32 * pi + 32, d, s_, ri, :], U[32 * pi:32 * pi + 32, s_, b * 512:(b + 1) * 512],
                                       s_ == 0, s_ == 7, [Bcst, BU], [Bp], tile_position=(32 * pi, 0))
                        T = []
                        for pi in range(4):
                            Sb, BSb = Sbr.next(); tw, Btw = twr.next(); Gt, BGt = Gtr.next(); Gs, BGs = Gsr.next()
                            (psr, Bpr), (psi, Bpi) = banks[pi]
                            cp("act", Sb[:, 0, :], psr[:, :], [Bpr], [BSb])
                            cp("act", Sb[:, 1, :], psi[:, :], [Bpi], [BSb])
                            tl = d * 4 + pi
                            tb = TBb[:, tl, :, :] if d == 0 else TBb[:, tl, :, ::-1]
                            T.append(dict(Sb=Sb, BSb=BSb, tw=tw, Btw=Btw, Gt=Gt, BGt=BGt, Gs=Gs, BGs=BGs, tl=tl,
                                          Fi=d * 16 + cc * 4 + pi, tb4=tb.unsqueeze(1).to_broadcast([128, 2, 2, 512])))
                        yield
                        for t in T:
                            tt("dve", t["tw"][:], t["Sb"][:].unsqueeze(2).to_broadcast([128, 2, 2, 512]), t["tb4"], ALU.mult, [t["BSb"], BTB], [t["Btw"]])
                        yield
                        for t in T:
                            tw, Gt = t["tw"], t["Gt"]
                            tt("dve", Gt[:, 0, :], tw[:, 0, 0, :], tw[:, 1, 1, :], ALU.add, [t["Btw"]], [t["BGt"]])
                            tt("dve", Gt[:, 1, :], tw[:, 1, 0, :], tw[:, 0, 1, :], ALU.subtract, [t["Btw"]], [t["BGt"]])
                            if masked:
                                tt("dve", Gt[:], Gt[:], c["Mskb"][:, d, b * 512:(b + 1) * 512].unsqueeze(1).to_broadcast([128, 2, 512]), ALU.mult,
                                   [t["BGt"], c["BMsk"]], [t["BGt"]])
                        yield
                        for pi, t in enumerate(T):
                            Fi, tl = t["Fi"], t["tl"]
                            if carry[pi] is None:
                                t["iv"] = (0.0, 0.0); t["rd"] = [t["BGt"], Bsc_c]
                            else:
                                c_fwd, c_swp, BGp = carry[pi]
                                ini, Bini = inir.next()
                                stt("dve", ini[:, 2:4], c_swp, EJ[:, 1, Fi:Fi + 1], sgn[:], ALU.mult, ALU.mult, [BGp, Bsc_c, Bsgn], [Bini])
                                stt("dve", ini[:, 0:2], c_fwd, EJ[:, 0, Fi:Fi + 1], ini[:, 2:4], ALU.mult, ALU.add, [BGp, Bsc_c, Bini], [Bini])
                                t["iv"] = (ini[:, 0:1], ini[:, 1:2]); t["rd"] = [t["BGt"], Bsc_c, Bini]
                                if rot and d == 0 and bidx == nblk:
                                    stt("dve", ini[:, 2:4], ini[:, 0:2][:, ::-1], E1[:, 1, Fi:Fi + 1], sgn2[:], ALU.mult, ALU.mult,
                                        [Bini, Bsc_c, Bsgn], [Bini])
                                    stt("dve", Hb[:, tl, :, 0], ini[:, 0:2], E1[:, 0, Fi:Fi + 1], ini[:, 2:4], ALU.mult, ALU.add,
                                        [Bini, Bsc_c], [Bini, BHb[tl]])
                        yield
                        for pi, t in enumerate(T):
                            Gs, Gt = t["Gs"], t["Gt"]
                            rdec = r_sc[:, t["Fi"]:t["Fi"] + 1].to_broadcast([128, 512])
                            for ri in range(2):
                                o_ = Gs[:, ri, :] if d == 0 else Gs[:, ri, ::-1]
                                i_ = Gt[:, ri, :] if d == 0 else Gt[:, ri, ::-1]
                                iv = t["iv"][ri]
                                S.op("dve", lambda e, o_=o_, i_=i_, iv=iv, rdec=rdec: e.tensor_tensor_scan(out=o_, data0=rdec, data1=i_, initial=iv,
                                                                                                      op0=ALU.mult, op1=ALU.add), t["rd"], [t["BGs"]])
                            last = 511 if d == 0 else 0
                            carry[pi] = (Gs[:, :, last], Gs[:, ::-1, last], t["BGs"])
                        yield
                        if not stored:
                            continue
                        for t in T:
                            tt("dve", t["tw"][:], t["Gs"][:].unsqueeze(2).to_broadcast([128, 2, 2, 512]), t["tb4"], ALU.mult, [t["BGs"], BTB], [t["Btw"]])
                        yield
                        c0 = 1 + b * 512
                        for t in T:
                            tw, tl = t["tw"], t["tl"]
                            tt("dve", Hb[:, tl, 0, c0:c0 + 512], tw[:, 0, 0, :], tw[:, 1, 1, :], ALU.subtract, [t["Btw"]], [BHb[tl]])
                            tt("dve", Hb[:, tl, 1, c0:c0 + 512], tw[:, 1, 0, :], tw[:, 0, 1, :], ALU.add, [t["Btw"]], [BHb[tl]])
                        yield

                def out_gen(c):
                    q, nco, U, BU, Hb, BHb = c["q"], c["nco"], c["U"], c["BU"], c["Hb"], c["BHb"]
                    for b in range(c["nblk_o"]):
                        ys, Bys = c["Yst"].next()
                        ysv = ys[:].rearrange("p (j t) -> p t j", t=8)
                        for t_ in range(8):
                            ps, Bp = PS.next()
                            for s_ in range(8):
                                mm(ps[:, 0:nco], KMt[:, t_ - s_ + 7, :], U[:, s_, b * nco:(b + 1) * nco], s_ == 0, False, [Bcst, BU], [Bp])
                            cnt = 0
                            for d in range(2):
                                c0 = b * nco + (0 if d == 0 else 2)
                                for pi in range(4):
                                    tl = d * 4 + pi
                                    for ri in range(2):
                                        cnt += 1
                                        mm(ps[32 * pi:32 * pi + 32, 0:nco], OUTm[:, d, pi, t_, ri, :], Hb[:, tl, ri, c0:c0 + nco],
                                           False, cnt == 16, [Bcst, BHb[tl]], [Bp], tile_position=(0, 32 * pi))
                            act(ysv[:, t_, :], ps[:, 0:nco], AF.Gelu_apprx_tanh, [Bp], [Bys])
                            yield
                        S.dma("sp", GS[q][cc * 128:(cc + 1) * 128, b * nco * 8:(b + 1) * nco * 8], ys[:], reads=[Bys])

                def step(g):
                    try:
                        next(g)
                        return True
                    except StopIteration:
                        return False

                pending = None
                for c in ctxs:
                    for d in range(2):
                        g_ = dir_gen(c, d)
                        while step(g_):
                            if pending is not None and not step(pending):
                                pending = None
                    while pending is not None:
                        if not step(pending):
                            pending = None
                    pending = out_gen(c)
                while pending is not None:
                    if not step(pending):
                        pending = None

        if 'S' not in skip:
            rot_qs = [q for q in range(nseq) if louts[q] < seqs[q]]
            oth_qs = [q for q in range(nseq) if louts[q] == seqs[q]]
            for cc in range(4):
                for q in rot_qs:
                    bs_group([q], cc)
                for i in range(0, len(oth_qs), 2):
                    bs_group(oth_qs[i:i + 2], cc)
        S.barrier()
        with ExitStack() as st:
            wglu = SB(st, "wglu", [128, 4, 1024], BF16); wout = SB(st, "wout", [128, 8, 1024], BF16); Bw1 = Buf("w_c1")
            S.dma("sp", wglu[:], wb_glu.rearrange("(k p) n -> p k n", p=128), reads=[Bwb["w_glu"]], writes=[Bw1])
            S.dma("sp", wout[:], wb_out.rearrange("(k p) n -> p k n", p=128), reads=[Bwb["w_out"]], writes=[Bw1])
            xin = Rot([(SB(st, "c1x%d" % i, [128, 8, 512]), Buf()) for i in range(2)])
            yfin = Rot([(SB(st, "c1yf%d" % i, [128, 4, 512], BF16), Buf()) for i in range(2)])
            gsin = Rot([(SB(st, "c1gs%d" % i, [128, 4, 512], BF16), Buf()) for i in range(2)])
            valr = Rot([(SB(st, "val%d" % i, [128, 4, 512]), Buf()) for i in range(2)])
            sgr = Rot([(SB(st, "sg%d" % i, [128, 4, 512]), Buf()) for i in range(2)])
            sqbr = Rot([(SB(st, "sqb%d" % i, [128, 8, 512], BF16), Buf()) for i in range(2)])
            rsbr = Rot([(SB(st, "rsb%d" % i, [128, 2, 512]), Buf()) for i in range(2)])
            mgr = Rot([(SB(st, "mg%d" % i, [128, 8, 512], BF16), Buf()) for i in range(2)])
            x1o = Rot([(SB(st, "x1o%d" % i, [128, 8, 512]), Buf()) for i in range(2)])
            tiles1 = [] if 'C' in skip else [(q, ti) for q, L in enumerate(seqs) for ti in range(louts[q] // 512)]
            st1 = {}

            def c_load(i):
                q, ti = tiles1[i]
                t0 = ti * 512
                xt, Bx = xin.next(); yf, Byf = yfin.next(); gs, Bgs = gsin.next()
                S.dma("sp", yf[:], YFd[q].rearrange("(k p) t -> p k t", p=128)[:, :, t0:t0 + 512], writes=[Byf])
                S.dma("sp", gs[:], GS[q].rearrange("(k p) t -> p k t", p=128)[:, :, t0:t0 + 512], writes=[Bgs])
                S.dma("sp", xt[:], xT[q].rearrange("(k p) t -> p k t", p=128)[:, :, t0:t0 + 512], writes=[Bx])
                st1[i] = dict(xt=xt, Bx=Bx, yf=yf, Byf=Byf, gs=gs, Bgs=Bgs)

            def c_p1(i):
                d_ = st1[i]
                val, Bval = valr.next(); sg, Bsg = sgr.next(); sqb, Bsqb = sqbr.next()
                act(sqb[:, 0:4, :], d_["yf"][:], AF.Square, [d_["Byf"]], [Bsqb])
                for m in range(8):
                    ps, Bp = PS.next()
                    for k in range(4):
                        mm(ps[:, :], wglu[:, k, m * 128:(m + 1) * 128], d_["gs"][:, k, :], k == 0, k == 3, [Bw1, d_["Bgs"]], [Bp])
                    if m < 4:
                        act(val[:, m, :], ps[:, :], AF.Identity, [Bp, Bvecs], [Bval], bias=V("b_glu")[:, m:m + 1])
                    else:
                        act(sg[:, m - 4, :], ps[:, :], AF.Sigmoid, [Bp, Bvecs], [Bsg], bias=V("b_glu")[:, m:m + 1])
                tt("dve", val[:], val[:], sg[:], ALU.mult, [Bval, Bsg], [Bval])
                act(sqb[:, 4:8, :], val[:], AF.Square, [Bval], [Bsqb])
                d_.update(val=val, Bval=Bval, sqb=sqb, Bsqb=Bsqb)

            def c_p2(i):
                d_ = st1[i]
                rsb, Brsb = rsbr.next(); mg, Bmg = mgr.next()
                for br in range(2):
                    ps, Bp = PS.next()
                    for k in range(4):
                        mm(ps[:, :], ones_bf[:], d_["sqb"][:, br * 4 + k, :], k == 0, k == 3, [Bones, d_["Bsqb"]], [Bp])
                    act(rsb[:, br, :], ps[:, :], AF.Sqrt, [Bp], [Brsb], bias=EPS, scale=1.0 / 512)
                S.op("dve", lambda e: e.reciprocal(out=rsb[:], in_=rsb[:]), [Brsb], [Brsb])
                for k in range(4):
                    stt("dve", mg[:, k, :], d_["yf"][:, k, :], V("g_f")[:, k:k + 1], rsb[:, 0, :], ALU.mult, ALU.mult, [d_["Byf"], Bvecs, Brsb], [Bmg])
                    stt("dve", mg[:, 4 + k, :], d_["val"][:, k, :], V("g_s")[:, k:k + 1], rsb[:, 1, :], ALU.mult, ALU.mult, [d_["Bval"], Bvecs, Brsb], [Bmg])
                d_.update(mg=mg, Bmg=Bmg)

            def c_p3(i):
                q, ti = tiles1[i]
                d_ = st1.pop(i)
                xo, Bxo = x1o.next()
                for m in range(8):
                    ps, Bp = PS.next()
                    for k in range(8):
                        mm(ps[:, :], wout[:, k, m * 128:(m + 1) * 128], d_["mg"][:, k, :], k == 0, k == 7, [Bw1, d_["Bmg"]], [Bp])
                    stt("dve", xo[:, m, :], ps[:, :], mod[:, 16 + m, q:q + 1], d_["xt"][:, m, :], ALU.mult, ALU.add, [Bp, Bmod, d_["Bx"]], [Bxo])
                S.dma("sp", X1[q].rearrange("(k p) t -> p k t", p=128)[:, :, ti * 512:(ti + 1) * 512], xo[:], reads=[Bxo])

            n1 = len(tiles1)
            if n1:
                c_load(0)
                if n1 > 1:
                    c_load(1)
                c_p1(0); c_p2(0)
            for i in range(n1):
                if i + 1 < n1:
                    c_p1(i + 1)
                c_p3(i)
                if i + 1 < n1:
                    c_p2(i + 1)
                if i + 2 < n1:
                    c_load(i + 2)
        S.barrier()
        NT = 256
        with ExitStack() as st:
            wmi = SB(st, "wmi", [128, 8, 4096], BF16); wmo = SB(st, "wmo", [128, 32, 1024], BF16); Bw2 = Buf("w_c2")
            Bw2i = Buf("wmi"); Bw2o = Buf("wmo"); Bw2 = Bw2i
            S.dma("sp", wmi[:], wb_mi.rearrange("(k p) n -> p k n", p=128), reads=[Bwb["w_mi"]], writes=[Bw2i])
            for i in range(4):
                S.dma("sp", wmo[:, i * 8:(i + 1) * 8, :], wb_mo[i * 1024:(i + 1) * 1024, :].rearrange("(k p) n -> p k n", p=128), reads=[Bwb["w_mo"]], writes=[Bw2o])
            for m in range(32):
                ps, Bp = PS.next()
                for k in range(8):
                    mm(ps[:, 0:nseq], wmi[:, k, m * 128:(m + 1) * 128], sh[:, 8 + k, :], k == 0, k == 7, [Bw2i, Bsh], [Bp])
                act(mB[:, m, :], ps[:, 0:nseq], AF.Identity, [Bp, Bvecs], [BmB], bias=V("b_mlp_in")[:, m:m + 1])
            x1in = Rot([(SB(st, "c2x%d" % i, [128, 8, NT]), Buf()) for i in range(2)])
            x2o = Rot([(SB(st, "c2y%d" % i, [128, 8, NT]), Buf()) for i in range(2)])
            sq2r = Rot([(SB(st, "sq2_%d" % i, [128, 8, NT], BF16), Buf()) for i in range(2)])
            rs2r = Rot([(SB(st, "rs2_%d" % i, [128, NT]), Buf()) for i in range(2)])
            h2r = Rot([(SB(st, "h2_%d" % i, [128, 8, NT], BF16), Buf()) for i in range(2)])
            aa = SB(st, "aa", [128, 32, NT], BF16); Baa = [Buf() for _ in range(32)]
            rl = Rot([(SB(st, "rl%d" % i, [128, NT]), Buf()) for i in range(3)])
            tiles2 = [] if 'D' in skip else [(q, ti) for q, L in enumerate(seqs) for ti in range(louts[q] // NT)]
            st2 = {}

            def d_load(i):
                q, ti = tiles2[i]
                x1, Bx1 = x1in.next()
                S.dma("sp", x1[:], X1[q].rearrange("(k p) t -> p k t", p=128)[:, :, ti * NT:(ti + 1) * NT], writes=[Bx1])
                st2[i] = dict(x1=x1, Bx1=Bx1)

            def d_square(i):
                d_ = st2[i]
                sq, Bsq = sq2r.next()
                act(sq[:], d_["x1"][:], AF.Square, [d_["Bx1"]], [Bsq])
                d_.update(sq=sq, Bsq=Bsq)

            def d_norm(i):
                q, ti = tiles2[i]
                d_ = st2[i]
                rs, Brs = rs2r.next(); h2, Bh2 = h2r.next()
                ps, Bp = PS.next()
                for k in range(8):
                    mm(ps[:, 0:NT], ones_bf[:], d_["sq"][:, k, :], k == 0, k == 7, [Bones, d_["Bsq"]], [Bp])
                act(rs[:], ps[:, 0:NT], AF.Sqrt, [Bp], [Brs], bias=EPS, scale=1.0 / 1024)
                S.op("dve", lambda e: e.reciprocal(out=rs[:], in_=rs[:]), [Brs], [Brs])
                x1, Bx1 = d_["x1"], d_["Bx1"]
                for k in range(8):
                    stt("dve", h2[:, k, :], x1[:, k, :], A2[:, k, q:q + 1], rs[:], ALU.mult, ALU.mult, [Bx1, BA2, Brs], [Bh2])
                for m in range(8):
                    act(x1[:, m, :], x1[:, m, :], AF.Identity, [Bx1, Bbg2, Bh2], [Bx1], bias=bg2[:, m, q:q + 1])
                d_.update(h2=h2, Bh2=Bh2)

            def d_mlp_in(i):
                q, ti = tiles2[i]
                d_ = st2[i]
                for f_ in range(32):
                    ps, Bp = PS.next()
                    for k in range(8):
                        mm(ps[:, 0:NT], wmi[:, k, f_ * 128:(f_ + 1) * 128], d_["h2"][:, k, :], k == 0, k == 7, [Bw2, d_["Bh2"]], [Bp])
                    rt, Brt = rl.next()
                    act(rt[:], ps[:, 0:NT], AF.Relu, [Bp, BmB], [Brt], bias=mB[:, f_, q:q + 1])
                    tt("dve" if f_ % 4 == 3 else "pool", aa[:, f_, :], rt[:], rt[:], ALU.mult, [Brt], [Baa[f_]])

            def d_mlp_out(i):
                q, ti = tiles2[i]
                d_ = st2[i]
                x2, Bx2 = x2o.next()
                for m in range(8):
                    ps, Bp = PS.next()
                    for f_ in range(32):
                        mm(ps[:, 0:NT], wmo[:, f_, m * 128:(m + 1) * 128], aa[:, f_, :], f_ == 0, f_ == 31, [Bw2o, Baa[f_]], [Bp])
                    stt("dve", x2[:, m, :], ps[:, 0:NT], mod[:, 40 + m, q:q + 1], d_["x1"][:, m, :], ALU.mult, ALU.add, [Bp, Bmod, d_["Bx1"]], [Bx2])
                d_.update(x2=x2, Bx2=Bx2)

            def d_tail_sq(i):
                d_ = st2[i]
                sq, Bsq = sq2r.next()
                act(sq[:], d_["x2"][:], AF.Square, [d_["Bx2"]], [Bsq])
                d_.update(sqf=sq, Bsqf=Bsq)

            def d_tail(i):
                q, ti = tiles2[i]
                d_ = st2.pop(i)
                x2, Bx2 = d_["x2"], d_["Bx2"]
                rs, Brs = rs2r.next()
                ps, Bp = PS.next()
                for k in range(8):
                    mm(ps[:, 0:NT], ones_bf[:], d_["sqf"][:, k, :], k == 0, k == 7, [Bones, d_["Bsqf"]], [Bp])
                act(rs[:], ps[:, 0:NT], AF.Sqrt, [Bp], [Brs], bias=EPS, scale=1.0 / 1024)
                S.op("dve", lambda e: e.reciprocal(out=rs[:], in_=rs[:]), [Brs], [Brs])
                for m in range(8):
                    stt("dve", x2[:, m, :], x2[:, m, :], V("g_final")[:, m:m + 1], rs[:], ALU.mult, ALU.mult, [Bx2, Bvecs, Brs], [Bx2])
                S.dma("sp", yT[q].rearrange("(k p) t -> p k t", p=128)[:, :, ti * NT:(ti + 1) * NT], x2[:], reads=[Bx2])

            n2 = len(tiles2)
            if n2:
                d_load(0); d_square(0); d_norm(0)
            for i in range(n2):
                if i + 1 < n2:
                    d_load(i + 1)
                if i > 0:
                    d_tail_sq(i - 1)
                if i + 1 < n2:
                    d_square(i + 1)
                d_mlp_in(i)
                if i + 1 < n2:
                    d_norm(i + 1)
                if i > 0:
                    d_tail(i - 1)
                d_mlp_out(i)
            if n2:
                d_tail_sq(n2 - 1); d_tail(n2 - 1)
        S.finish()
    print("n_instr", S.n_instr, {k: v.count for k, v in S.E.items()})
    return nc


def make_inmap(inp, com, vecs, lay, xs, cs):
    nseq = len(xs)
    v = vecs.copy()
    o, w = lay["cT"]
    cT = np.stack([_pk(c) for c in cs], axis=2)
    v[:, o:o + w] = cT.reshape(128, 8 * nseq)
    m = dict(com)
    m["vecs"] = v
    for q, x in enumerate(xs):
        m["xT%d" % q] = np.ascontiguousarray(x.T)
        L = x.shape[0]
        for k, a in _dft_consts(L).items():
            m["%s_q%d" % (k, q)] = a
    return m


def prompt_inputs(xp, i, L=16384, Lout=2048):
    s = Lout * i
    d = {"xT0": np.ascontiguousarray(np.roll(xp, -s, axis=0).T)}
    for k, a in _dft_consts(L, rot_i=i, Lout=Lout).items():
        d["%s_q0" % k] = a
    NC = L // 8
    j0 = (L - s) // 8
    j = np.arange(NC)
    msk = np.stack([(j >= j0), (j < j0)]).astype(np.float32)
    d["msk_q0"] = np.ascontiguousarray(np.broadcast_to(msk[None], (128, 2, NC))).astype(ml_dtypes.bfloat16)
    return d


SEQS = [(16384, 2048), (4096, 4096), (4096, 4096)]
_NC_CACHE = {}


def kernel(**inputs):
    inp = {k: np.asarray(v) for k, v in inputs.items()}
    nseq = len(SEQS)
    com, vecs, lay = _host_common(inp, nseq)
    if "nc" not in _NC_CACHE:
        _NC_CACHE["nc"] = build(SEQS)
    nc = _NC_CACHE["nc"]
    consts = {}
    for k, a in _dft_consts(4096).items():
        consts["%s_q1" % k] = a
        consts["%s_q2" % k] = a
    o, w = lay["cT"]
    in_maps = []
    for i in range(8):
        cs = [inp["c_prompt"][0], inp["c_sample"][2 * i], inp["c_sample"][2 * i + 1]]
        v = vecs.copy()
        v[:, o:o + w] = np.stack([_pk(c) for c in cs], axis=2).reshape(128, 8 * nseq)
        m = dict(com)
        m.update(consts)
        m["vecs"] = v
        m.update(prompt_inputs(inp["x_prompt"][0], i))
        m["xT1"] = np.ascontiguousarray(inp["x_sample"][2 * i].T)
        m["xT2"] = np.ascontiguousarray(inp["x_sample"][2 * i + 1].T)
        in_maps.append(m)
    res = run_bass_kernel_spmd(nc, in_maps, core_ids=list(range(8)))
    R = res.results
    y_prompt = np.concatenate([np.asarray(R[i]["yT0"]).T for i in range(8)], axis=0)[None].astype(np.float32)
    y_sample = np.stack([np.ascontiguousarray(np.asarray(R[i]["yT%d" % (1 + j)]).T) for i in range(8) for j in range(2)]).astype(np.float32)
    return (y_prompt, y_sample)
```

```python
import math
from contextlib import ExitStack
import numpy as np
import ml_dtypes
import concourse.bass as bass
import concourse.mybir as mybir
from concourse.bass_utils import run_bass_kernel_spmd

F32 = mybir.dt.float32
BF16 = mybir.dt.bfloat16
ALU = mybir.AluOpType
AF = mybir.ActivationFunctionType
EPS = 1e-6
JB = 128
JB2 = 512


class _Eng:
    def __init__(self, name, eng, sem):
        self.name, self.eng, self.sem = name, eng, sem
        self.count = 0
        self.waited = {}


class Buf:
    __slots__ = ("name", "w", "r")

    def __init__(self, name=""):
        self.name = name
        self.w = None
        self.r = []


class Sched:
    def __init__(self, nc, stack, n_dma_slots=10, same_engine_sync=True):
        self.nc = nc
        self.E = {}
        for name, eng in (("pe", nc.tensor), ("act", nc.scalar), ("dve", nc.vector),
                          ("pool", nc.gpsimd), ("sp", nc.sync)):
            sem = stack.enter_context(nc.semaphore("s_" + name))
            self.E[name] = _Eng(name, eng, sem)
        self.same = same_engine_sync
        self.slots = {}
        for q in ("sp", "act", "pool"):
            sl = []
            for i in range(n_dma_slots):
                sem = stack.enter_context(nc.semaphore("d_%s%d" % (q, i)))
                sl.append([sem, 0])
            self.slots[q] = [sl, 0]
        self.n_instr = 0

    def _wait(self, e, sem, val):
        key = id(sem)
        if e.waited.get(key, 0) >= val:
            return
        e.eng.wait_ge(sem, val)
        e.waited[key] = val

    def _need(self, e, ev):
        if ev[2] == e.name:
            if e.name == "pe":
                return False
            if isinstance(self.same, (set, frozenset, tuple, list)):
                return e.name in self.same
            return self.same
        return True

    def _deps(self, e, reads, writes):
        for b in reads:
            if b.w is not None and self._need(e, b.w):
                self._wait(e, b.w[0], b.w[1])
        for b in writes:
            if b.w is not None and self._need(e, b.w):
                self._wait(e, b.w[0], b.w[1])
            for ev in b.r:
                if self._need(e, ev):
                    self._wait(e, ev[0], ev[1])

    def op(self, engname, fn, reads=(), writes=()):
        e = self.E[engname]
        self._deps(e, reads, writes)
        ins = fn(e.eng)
        e.count += 1
        ins.then_inc(e.sem, 1)
        ev = (e.sem, e.count, e.name)
        for b in reads:
            b.r.append(ev)
            if len(b.r) > 24:
                b.r = b.r[-24:] if False else b.r
        for b in writes:
            b.w = ev
            b.r = []
        self.n_instr += 1
        return ins

    def dma(self, q, out, in_, reads=(), writes=(), **kw):
        e = self.E[q]
        self._deps(e, reads, writes)
        sl, idx = self.slots[q]
        slot = sl[idx % len(sl)]
        self.slots[q][1] = idx + 1
        if slot[1] > 0:
            self._wait(e, slot[0], slot[1])
        ins = e.eng.dma_start(out=out, in_=in_, **kw)
        slot[1] += 16
        ins.then_inc(slot[0], 16)
        ev = (slot[0], slot[1], "dma_" + q)
        for b in reads:
            b.r.append(ev)
        for b in writes:
            b.w = ev
            b.r = []
        self.n_instr += 1
        return ins

    def barrier(self):
        names = ("pe", "act", "dve", "pool", "sp")
        for n in names:
            e = self.E[n]
            for q in self.slots:
                for slot in self.slots[q][0]:
                    if slot[1] > 0:
                        self._wait(e, slot[0], slot[1])
            for o in names:
                if o != n and self.E[o].count > 0:
                    self._wait(e, self.E[o].sem, self.E[o].count)

    def finish(self):
        e = self.E["sp"]
        for q in self.slots:
            for slot in self.slots[q][0]:
                if slot[1] > 0:
                    self._wait(e, slot[0], slot[1])
        for n in ("pe", "act", "dve", "pool"):
            o = self.E[n]
            if o.count > 0:
                self._wait(e, o.sem, o.count)


class Rot:
    def __init__(self, items):
        self.items = items
        self.i = 0

    def next(self):
        it = self.items[self.i % len(self.items)]
        self.i += 1
        return it


VEC_LAYOUT = {}


def _vec_layout(nseq):
    off = 0
    lay = {}
    for name, w in (("b_ada", 48), ("g_mix", 8), ("g_mlp", 8), ("g_final", 8), ("b_glu", 8),
                    ("b_mlp_out", 8), ("b_mlp_in", 32), ("g_f", 4), ("g_s", 4), ("b_f", 4), ("ssm_d", 4),
                    ("m0H", 1), ("m1H", 1), ("m0S", 1), ("m1S", 1), ("nm0S", 1), ("nm1S", 1),
                    ("cT", 8 * nseq)):
        lay[name] = (off, w)
        off += w
    return lay, off


def _pk(v):
    v = np.asarray(v, np.float32).reshape(-1, 128)
    return np.ascontiguousarray(v.T)


def _dft_consts(L, rot_i=None, Lout=None):
    N1 = L // 128
    n1 = np.arange(N1)
    ang = 2 * np.pi * ((n1[:, None] * n1[None, :]) % N1) / N1
    Fc, Fs = np.cos(ang), np.sin(ang)
    FF1 = np.concatenate([Fc, -Fs], axis=1)
    FF2 = np.concatenate([-Fs, -Fc], axis=1)
    n2 = np.arange(128, dtype=np.int64)
    if rot_i is None:
        shift, k0, Lo = 0, 0, L
    else:
        shift, k0, Lo = Lout * rot_i, Lout * rot_i, Lout
    k = k0 + np.arange(Lo, dtype=np.int64)
    angw = 2 * np.pi * (((n2[:, None] + shift) * k[None, :]) % L) / L
    sc = 1.0 / math.sqrt(512.0 * L)
    f = lambda a: np.ascontiguousarray(a, dtype=np.float32).astype(ml_dtypes.bfloat16)
    return dict(FF1=f(FF1), FF2=f(FF2), Wc=f(np.cos(angw) * sc), Ws=f(np.sin(angw) * sc))


def _host_common(inp, nseq):
    lay, nv = _vec_layout(nseq)
    vecs = np.zeros((128, nv), np.float32)

    def put(name, arr):
        o, w = lay[name]
        vecs[:, o:o + w] = arr

    put("b_ada", _pk(inp["b_ada"][0]))
    put("g_mix", _pk(inp["g_mix_norm"][0]))
    put("g_mlp", _pk(inp["g_mlp_norm"][0]))
    put("g_final", _pk(inp["g_final"]))
    put("b_glu", _pk(inp["b_glu"][0]))
    put("b_mlp_out", _pk(inp["b_mlp_out"][0]))
    put("b_mlp_in", _pk(inp["b_mlp_in"][0]))
    put("g_f", _pk(inp["g_fourier_out"][0]))
    put("g_s", _pk(inp["g_ssm_out"][0]))
    put("b_f", _pk(inp["b_fourier"][0]))
    put("ssm_d", _pk(inp["ssm_d"][0]))
    p = np.arange(128)
    m0H = (((p // 16) % 2) == 0).astype(np.float32)
    m0S = (p < 64).astype(np.float32)
    put("m0H", m0H[:, None]); put("m1H", (1 - m0H)[:, None])
    put("m0S", m0S[:, None]); put("m1S", (1 - m0S)[:, None])
    put("nm0S", -m0S[:, None]); put("nm1S", -(1 - m0S)[:, None])
    c = np.arange(512)
    angc = 2 * np.pi * ((c[:, None] * c[None, :]) % 512) / 512
    com = dict(
        w_ada=np.ascontiguousarray(inp["w_ada"][0]), w_in=np.ascontiguousarray(inp["w_in"][0]),
        w_glu=np.ascontiguousarray(inp["w_glu"][0]), w_out=np.ascontiguousarray(inp["w_out"][0]),
        w_mlp_in=np.ascontiguousarray(inp["w_mlp_in"][0]), w_mlp_out=np.ascontiguousarray(inp["w_mlp_out"][0]),
        w_f=np.ascontiguousarray(inp["w_fourier"][0]),
        CC=np.cos(angc).astype(np.float32), SC=np.sin(angc).astype(np.float32),
        ident=np.eye(128, dtype=np.float32),
    )
    a_re, a_im, ldt = inp["ssm_a_re"][0], inp["ssm_a_im"][0], inp["ssm_log_dt"][0]
    b_re, b_im, c_re, c_im = inp["ssm_b_re"][0], inp["ssm_b_im"][0], inp["ssm_c_re"][0], inp["ssm_c_im"][0]

    def hmaj_a(a):
        t = a.reshape(2, 4, 8, 64)
        t = np.broadcast_to(t[:, :, :, None, :], (2, 4, 8, 16, 64))
        return np.ascontiguousarray(t.transpose(2, 3, 0, 1, 4).reshape(128, 2, 4, 64), dtype=np.float32)

    def hmaj_b(b):
        t = b.reshape(2, 4, 8, 64, 16)
        return np.ascontiguousarray(t.transpose(2, 4, 0, 1, 3).reshape(128, 2, 4, 64), dtype=np.float32)

    def smaj_a(a):
        t = a.reshape(2, 4, 4, 2, 64)
        return np.ascontiguousarray(t.transpose(3, 4, 0, 1, 2).reshape(128, 2, 4, 4), dtype=np.float32)

    def smaj_c(cc_):
        t = cc_.reshape(2, 4, 4, 2, 16, 64)
        return np.ascontiguousarray(t.transpose(3, 5, 0, 1, 2, 4).reshape(128, 2, 4, 4, 16), dtype=np.float32)

    def smaj_b(b):
        t = b.reshape(2, 4, 4, 2, 64, 16)
        return np.ascontiguousarray(t.transpose(3, 4, 0, 1, 2, 5).reshape(128, 2, 4, 4, 16), dtype=np.float32)

    ldt3 = np.broadcast_to(ldt[:, :, None], (2, 32, 64))
    com.update(
        aH=np.stack([hmaj_a(a_re), hmaj_a(a_im), hmaj_a(ldt3)], axis=1),
        bH=np.stack([hmaj_b(b_re), hmaj_b(b_im)], axis=1),
        aS=np.stack([smaj_a(a_re), smaj_a(a_im), smaj_a(ldt3)], axis=1),
        cS=np.stack([smaj_c(c_re), smaj_c(c_im)], axis=1),
        bS=np.stack([smaj_b(b_re), smaj_b(b_im)], axis=1),
    )
    return com, vecs, lay


def build(seqs, dbg=False, same=('pool', 'dve'), skip=''):
    seqs_full = [(a, a) if isinstance(a, int) else tuple(a) for a in seqs]
    louts = [b for a, b in seqs_full]
    seqs = [a for a, b in seqs_full]
    nseq = len(seqs)
    lay, nv = _vec_layout(nseq)
    nc = bass.Bass("TRN2", target_bir_lowering=False)
    DI = lambda name, shape, dt=F32: nc.dram_tensor(name, list(shape), dt, kind="ExternalInput").ap()
    DO = lambda name, shape, dt=F32: nc.dram_tensor(name, list(shape), dt, kind="ExternalOutput").ap()
    DS = lambda name, shape, dt=F32: nc.dram_tensor(name, list(shape), dt, kind=("ExternalOutput" if dbg else "Internal")).ap()

    xT = [DI("xT%d" % q, [1024, L]) for q, L in enumerate(seqs)]
    yT = [DO("yT%d" % q, [1024, louts[q]]) for q, L in enumerate(seqs)]
    vecs_d = DI("vecs", [128, nv])
    w_ada_d = DI("w_ada", [1024, 6144]); w_in_d = DI("w_in", [1024, 1024]); w_glu_d = DI("w_glu", [512, 1024])
    w_out_d = DI("w_out", [1024, 1024]); w_mi_d = DI("w_mlp_in", [1024, 4096]); w_mo_d = DI("w_mlp_out", [4096, 1024])
    w_f_d = DI("w_f", [4, 128, 128]); CC_d = DI("CC", [512, 512]); SC_d = DI("SC", [512, 512]); ident_d = DI("ident", [128, 128])
    aH_d = DI("aH", [128, 3, 2, 4, 64]); bH_d = DI("bH", [128, 2, 2, 4, 64]); aS_d = DI("aS", [128, 3, 2, 4, 4])
    cS_d = DI("cS", [128, 2, 2, 4, 4, 16]); bS_d = DI("bS", [128, 2, 2, 4, 4, 16])
    dftc = {}
    mskd = {}
    for q, L in enumerate(seqs):
        N1 = L // 128
        nk2 = louts[q] // N1
        dftc[q] = dict(FF1=DI("FF1_q%d" % q, [N1, 2 * N1], BF16), FF2=DI("FF2_q%d" % q, [N1, 2 * N1], BF16),
                       Wc=DI("Wc_q%d" % q, [128, louts[q]], BF16), Ws=DI("Ws_q%d" % q, [128, louts[q]], BF16))
        if louts[q] < L:
            mskd[q] = DI("msk_q%d" % q, [128, 2, L // 8], BF16)
    AB = [DS("AB%d" % q, [4, L, 256], BF16) for q, L in enumerate(seqs)]
    US = [DS("US%d" % q, [512, L], BF16) for q, L in enumerate(seqs)]
    YFd = [DS("YF%d" % q, [512, louts[q]], BF16) for q, L in enumerate(seqs)]
    GS = [DS("GS%d" % q, [512, louts[q]], BF16) for q, L in enumerate(seqs)]
    X1 = [DS("X1%d" % q, [1024, louts[q]], F32) for q, L in enumerate(seqs)]
    wb_in = DS("wb_in", [1024, 1024], BF16); wb_glu = DS("wb_glu", [512, 1024], BF16); wb_out = DS("wb_out", [1024, 1024], BF16)
    wb_mi = DS("wb_mi", [1024, 4096], BF16); wb_mo = DS("wb_mo", [4096, 1024], BF16)
    INd = DS("INd", [128, 4, 2, 8, 2, 128], BF16)
    OUTd = DS("OUTd", [128, 4, 2, 4, 8, 2, 32], BF16)
    KMd = DS("KMd", [128, 4, 15, 128], BF16)
    TBd = DS("TBd", [128, 4, 8, 2, JB2], BF16)
    MODd = DS("MODd", [128, 48, nseq], F32) if dbg else None
    Gd = DS("Gd", [128, 4, 1024], BF16) if dbg else None
    Zd = DS("Zd", [128, 8, 512], BF16) if dbg else None
    ABd = DS("ABd", [128, 4, 1024], BF16) if dbg else None

    with ExitStack() as top:
        S = Sched(nc, top, same_engine_sync=same)
        _cnt = [0]

        def SB(st, name, shape, dt=F32):
            _cnt[0] += 1
            return st.enter_context(nc.sbuf_tensor("sb%d_%s" % (_cnt[0], name), list(shape), dt))
        PS = Rot([(top.enter_context(nc.psum_tensor("ps%d" % i, [128, 512], F32)), Buf("ps%d" % i)) for i in range(8)])

        def mm(out, lhsT, rhs, start, stop, reads, writes, **kw):
            return S.op("pe", lambda e: e.matmul(out, lhsT=lhsT, rhs=rhs, start=start, stop=stop, **kw), reads, writes)

        def act(out, in_, func, reads, writes, **kw):
            return S.op("act", lambda e: e.activation(out=out, in_=in_, func=func, **kw), reads, writes)

        def tt(eng, out, in0, in1, op, reads, writes):
            return S.op(eng, lambda e: e.tensor_tensor(out=out, in0=in0, in1=in1, op=op), reads, writes)

        def ts(eng, out, in0, s1, s2, op0, op1, reads, writes):
            if s2 is None:
                return S.op(eng, lambda e: e.tensor_scalar(out=out, in0=in0, scalar1=s1, scalar2=None, op0=op0), reads, writes)
            return S.op(eng, lambda e: e.tensor_scalar(out=out, in0=in0, scalar1=s1, scalar2=s2, op0=op0, op1=op1), reads, writes)

        def stt(eng, out, in0, scalar, in1, op0, op1, reads, writes):
            return S.op(eng, lambda e: e.scalar_tensor_tensor(out=out, in0=in0, scalar=scalar, in1=in1, op0=op0, op1=op1), reads, writes)

        def cp(eng, out, in_, reads, writes):
            if eng == "act":
                return act(out, in_, AF.Identity, reads, writes)
            return S.op(eng, lambda e: e.tensor_copy(out=out, in_=in_), reads, writes)

        vecs = SB(top, "vecs", [128, nv]); Bvecs = Buf("vecs")
        S.dma("sp", vecs[:], vecs_d[:, :], writes=[Bvecs])
        V = lambda name: vecs[:, lay[name][0]:lay[name][0] + lay[name][1]]
        Bwb = dict(w_in=Buf(), w_glu=Buf(), w_out=Buf(), w_mi=Buf(), w_mo=Buf())
        S.dma("pool", wb_in[:, :], w_in_d[:, :], writes=[Bwb["w_in"]])
        ones_bf = SB(top, "ones_bf", [128, 128], BF16); Bones = Buf("ones")
        S.op("pool", lambda e: e.memset(ones_bf[:], 1.0), writes=[Bones])
        ident = SB(top, "ident", [128, 128]); Bident = Buf("ident")
        S.dma("sp", ident[:], ident_d[:, :], writes=[Bident])
        mod = SB(top, "mod", [128, 48, nseq]); Bmod = Buf("mod")
        A1 = SB(top, "A1", [128, 8, nseq]); A2 = SB(top, "A2", [128, 8, nseq]); bg2 = SB(top, "bg2", [128, 8, nseq])
        zB = SB(top, "zB", [128, 8, nseq]); mB = SB(top, "mB", [128, 32, nseq])
        sh = SB(top, "sh_bf", [128, 16, nseq], BF16); Bsh = Buf("sh")
        BA1, BA2, Bbg2, BzB, BmB = Buf("A1"), Buf("A2"), Buf("bg2"), Buf("zB"), Buf("mB")
        r_sc = SB(top, "r_sc", [128, 32]); EJ = SB(top, "EJ", [128, 2, 32]); E1 = SB(top, "E1", [128, 2, 32]); Bsc_c = Buf("scanconst")
        stG = ExitStack()
        Gcat = SB(stG, "Gcat", [128, 4, 1024], BF16); BG = Buf("Gcat")

        stbox = [None]
        if True:
            def cmul(o_re, o_im, a_re, a_im, b_re, b_im, t1, t2, Bs):
                tt("dve", t1, a_re, b_re, ALU.mult, Bs, Bs); tt("dve", t2, a_im, b_im, ALU.mult, Bs, Bs)
                tt("dve", o_re, t1, t2, ALU.subtract, Bs, Bs)
                tt("dve", t1, a_re, b_im, ALU.mult, Bs, Bs); tt("dve", t2, a_im, b_re, ALU.mult, Bs, Bs)
                tt("dve", o_im, t1, t2, ALU.add, Bs, Bs)

            def cx_setup(name, a3, F, Bs):
                W = SB(stbox[0], name + "_w", [128, 12, F])
                PW = SB(stbox[0], name + "_pw", [128, 9, 2, F])
                lam, dt, x16, th, m, sn, cs_, t1, t2, cr, ci, den = [W[:, i, :] for i in range(12)]
                ts("dve", lam, a3[:, 0, :], -1e-4, None, ALU.min, None, Bs, Bs)
                act(dt, a3[:, 2, :], AF.Exp, Bs, Bs)
                tt("dve", x16, lam, dt, ALU.mult, Bs, Bs); ts("dve", x16, x16, 1.0 / 16, None, ALU.mult, None, Bs, Bs)
                tt("dve", th, a3[:, 1, :], dt, ALU.mult, Bs, Bs); ts("dve", th, th, 1.0 / 16, None, ALU.mult, None, Bs, Bs)
                ts("dve", m, x16, 1.0 / 6, 0.5, ALU.mult, ALU.add, Bs, Bs)
                tt("dve", m, m, x16, ALU.mult, Bs, Bs); ts("dve", m, m, 1.0, None, ALU.add, None, Bs, Bs)
                tt("dve", m, m, x16, ALU.mult, Bs, Bs); ts("dve", m, m, 1.0, None, ALU.add, None, Bs, Bs)
                act(sn, th, AF.Sin, Bs, Bs)
                act(cs_, th, AF.Sin, Bs, Bs, bias=math.pi / 2)
                are, aim = PW[:, 1, 0, :], PW[:, 1, 1, :]
                tt("dve", are, m, cs_, ALU.mult, Bs, Bs); tt("dve", aim, m, sn, ALU.mult, Bs, Bs)
                for _ in range(4):
                    tt("dve", t1, are, are, ALU.mult, Bs, Bs); tt("dve", t2, aim, aim, ALU.mult, Bs, Bs)
                    tt("dve", aim, are, aim, ALU.mult, Bs, Bs); ts("dve", aim, aim, 2.0, None, ALU.mult, None, Bs, Bs)
                    tt("dve", are, t1, t2, ALU.subtract, Bs, Bs)
                S.op("dve", lambda e: e.memset(PW[:, 0, 0, :], 1.0), Bs, Bs)
                S.op("dve", lambda e: e.memset(PW[:, 0, 1, :], 0.0), Bs, Bs)
                for k in range(2, 9):
                    cmul(PW[:, k, 0, :], PW[:, k, 1, :], PW[:, k - 1, 0, :], PW[:, k - 1, 1, :], are, aim, t1, t2, Bs)
                nr = x16
                ts("dve", nr, are, -1.0, None, ALU.add, None, Bs, Bs)
                tt("dve", t1, lam, lam, ALU.mult, Bs, Bs); tt("dve", t2, a3[:, 1, :], a3[:, 1, :], ALU.mult, Bs, Bs)
                tt("dve", den, t1, t2, ALU.add, Bs, Bs)
                S.op("dve", lambda e: e.reciprocal(out=den, in_=den), Bs, Bs)
                tt("dve", t1, nr, lam, ALU.mult, Bs, Bs); tt("dve", t2, aim, a3[:, 1, :], ALU.mult, Bs, Bs)
                tt("dve", cr, t1, t2, ALU.add, Bs, Bs); tt("dve", cr, cr, den, ALU.mult, Bs, Bs)
                tt("dve", t1, aim, lam, ALU.mult, Bs, Bs); tt("dve", t2, nr, a3[:, 1, :], ALU.mult, Bs, Bs)
                tt("dve", ci, t1, t2, ALU.subtract, Bs, Bs); tt("dve", ci, ci, den, ALU.mult, Bs, Bs)
                return PW, cr, ci, t1, t2

        stH = ExitStack()
        st = stH
        stbox[0] = st
        BH = [Buf("ssmH")]
        aH = SB(st, "aH", [128, 3, 512]); bH = SB(st, "bH", [128, 2, 512])
        S.dma("sp", aH[:], aH_d.rearrange("p a d c s -> p a (d c s)"), writes=BH)
        S.dma("sp", bH[:], bH_d.rearrange("p a d c s -> p a (d c s)"), writes=BH)
        PWH, crH, ciH, t1H, t2H = cx_setup("H", aH, 512, BH)
        bbH = SB(st, "bbH", [128, 2, 512])
        cmul(bbH[:, 0, :], bbH[:, 1, :], crH, ciH, bH[:, 0, :], bH[:, 1, :], t1H, t2H, BH)
        INm = SB(st, "INm", [128, 4, 2, 8, 2, 128], BF16)
        tmpH = SB(st, "tmpH", [128, 2, 256])
        for d in range(2):
            dsl = slice(d * 256, (d + 1) * 256)
            for s_ in range(8):
                e_ = 7 - s_ if d == 0 else s_
                cmul(tmpH[:, 0, :], tmpH[:, 1, :], PWH[:, e_, 0, dsl], PWH[:, e_, 1, dsl], bbH[:, 0, dsl], bbH[:, 1, dsl],
                     t1H[:, 0:256], t2H[:, 0:256], BH)
                for ri in range(2):
                    src = tmpH[:, ri, :].rearrange("p (c s) -> p c s", s=64)
                    ts("dve", INm[:, :, d, s_, ri, 0:64], src, V("m0H"), None, ALU.mult, None, BH + [Bvecs], BH)
                    ts("dve", INm[:, :, d, s_, ri, 64:128], src, V("m1H"), None, ALU.mult, None, BH + [Bvecs], BH)
        S.dma("sp", INd.rearrange("p c d s r k -> p (c d s r k)"), INm[:].rearrange("p c d s r k -> p (c d s r k)"), reads=BH)

        with ExitStack() as st:
            sc = SB(st, "silu_c", [128, 8 * nseq]); Bsc = Buf()
            act(sc[:], V("cT"), AF.Silu, [Bvecs], [Bsc])
            scv = sc[:].rearrange("p (k q) -> p k q", q=nseq)
            wa = [(SB(st, "wa%d" % i, [128, 8, 1024]), Buf()) for i in range(2)]
            for cb in range(6):
                wt, Bw = wa[cb % 2]
                S.dma("sp", wt[:], w_ada_d.rearrange("(k p) n -> p k n", p=128)[:, :, cb * 1024:(cb + 1) * 1024], writes=[Bw])
                for m in range(8):
                    ps, Bp = PS.next()
                    for k in range(8):
                        mm(ps[:, 0:nseq], wt[:, k, m * 128:(m + 1) * 128], scv[:, k, :], k == 0, k == 7, [Bw, Bsc], [Bp])
                    j = cb * 8 + m
                    act(mod[:, j, :], ps[:, 0:nseq], AF.Identity, [Bp, Bvecs], [Bmod], bias=V("b_ada")[:, j:j + 1])
            if dbg:
                S.dma("sp", MODd[:, :, :], mod[:], reads=[Bmod])
            ts("dve", A1[:], mod[:, 8:16, :], 1.0, None, ALU.add, None, [Bmod], [BA1])
            tt("dve", A1[:], A1[:], V("g_mix").unsqueeze(2).to_broadcast([128, 8, nseq]), ALU.mult, [BA1, Bvecs], [BA1])
            ts("dve", A2[:], mod[:, 32:40, :], 1.0, None, ALU.add, None, [Bmod], [BA2])
            tt("dve", A2[:], A2[:], V("g_mlp").unsqueeze(2).to_broadcast([128, 8, nseq]), ALU.mult, [BA2, Bvecs], [BA2])
            tt("dve", bg2[:], mod[:, 40:48, :], V("b_mlp_out").unsqueeze(2).to_broadcast([128, 8, nseq]), ALU.mult, [Bmod, Bvecs], [Bbg2])
            cp("dve", sh[:, 0:8, :], mod[:, 0:8, :], [Bmod], [Bsh])
            cp("dve", sh[:, 8:16, :], mod[:, 24:32, :], [Bmod], [Bsh])
            cs = SB(st, "cs", [128, 2, 4, 512], BF16); Bcs = Buf()
            S.dma("pool", cs[:, 0, :, :], CC_d.rearrange("(h k) c -> k h c", k=128), writes=[Bcs])
            S.dma("pool", cs[:, 1, :, :], SC_d.rearrange("(h k) c -> k h c", k=128), writes=[Bcs])
            wf = SB(st, "wf", [128, 4, 128], BF16); Bwf = Buf()
            S.dma("pool", wf[:], w_f_d.rearrange("h d e -> d h e"), writes=[Bwf])
            for h in range(4):
                for ri in range(2):
                    ps, Bp = PS.next()
                    for kc in range(4):
                        mm(ps[:, kc * 128:(kc + 1) * 128], cs[:, ri, h, kc * 128:(kc + 1) * 128], wf[:, h, :], True, True, [Bcs, Bwf], [Bp])
                    cp("act", Gcat[:, :, h * 256 + ri * 128: h * 256 + ri * 128 + 128],
                       ps[:, :].rearrange("p (k e) -> p k e", e=128), [Bp], [BG])

        if dbg:
            S.dma("sp", Gd[:, :, :], Gcat[:], reads=[BG])
        S.barrier()
        stH.close()
        with ExitStack() as st:
            w_in = SB(st, "w_in", [128, 8, 1024], BF16); Bwin = Buf()
            S.dma("sp", w_in[:], wb_in.rearrange("(k p) n -> p k n", p=128), reads=[Bwb["w_in"]], writes=[Bwin])
            for m in range(8):
                ps, Bp = PS.next()
                for k in range(8):
                    mm(ps[:, 0:nseq], w_in[:, k, m * 128:(m + 1) * 128], sh[:, k, :], k == 0, k == 7, [Bwin, Bsh], [Bp])
                cp("act", zB[:, m, :], ps[:, 0:nseq], [Bp], [BzB])
            xin = Rot([(SB(st, "xin%d" % i, [128, 8, 512]), Buf()) for i in range(2)])
            sqr = Rot([(SB(st, "sq%d" % i, [128, 8, 512], BF16), Buf()) for i in range(2)])
            rsr = Rot([(SB(st, "rs%d" % i, [128, 512]), Buf()) for i in range(2)])
            hbr = Rot([(SB(st, "hb%d" % i, [128, 8, 512], BF16), Buf()) for i in range(2)])
            zt = Rot([(SB(st, "zt%d" % i, [128, 8, 512], BF16), Buf()) for i in range(2)])
            abt = Rot([(SB(st, "abt%d" % i, [128, 4, 1024], BF16), Buf()) for i in range(2)])
            tiles = [] if 'A' in skip else [(q, ti) for q, L in enumerate(seqs) for ti in range(L // 512)]
            stA = {}

            def a_load(i):
                q, ti = tiles[i]
                xt, Bx = xin.next()
                S.dma("sp", xt[:], xT[q].rearrange("(k p) t -> p k t", p=128)[:, :, ti * 512:(ti + 1) * 512], writes=[Bx])
                stA[i] = dict(xt=xt, Bx=Bx)

            def a_square(i):
                d_ = stA[i]
                sq, Bsq = sqr.next()
                act(sq[:], d_["xt"][:], AF.Square, [d_["Bx"]], [Bsq])
                d_.update(sq=sq, Bsq=Bsq)

            def a_norm(i):
                q, ti = tiles[i]
                d_ = stA[i]
                rs, Brs = rsr.next(); hb, Bh = hbr.next()
                ps, Bp = PS.next()
                for k in range(8):
                    mm(ps[:, :], ones_bf[:], d_["sq"][:, k, :], k == 0, k == 7, [Bones, d_["Bsq"]], [Bp])
                act(rs[:], ps[:, :], AF.Sqrt, [Bp], [Brs], bias=EPS, scale=1.0 / 1024)
                S.op("dve", lambda e: e.reciprocal(out=rs[:], in_=rs[:]), [Brs], [Brs])
                for k in range(8):
                    stt("dve", hb[:, k, :], d_["xt"][:, k, :], A1[:, k, q:q + 1], rs[:], ALU.mult, ALU.mult, [d_["Bx"], BA1, Brs], [Bh])
                d_.update(hb=hb, Bh=Bh)

            def a_main1(i, half):
                q, ti = tiles[i]
                d_ = stA[i]
                if half == 0:
                    z, Bz = zt.next()
                    d_.update(z=z, Bz=Bz)
                z, Bz = d_["z"], d_["Bz"]
                for m in range(half * 4, half * 4 + 4):
                    ps, Bp = PS.next()
                    for k in range(8):
                        mm(ps[:, :], w_in[:, k, m * 128:(m + 1) * 128], d_["hb"][:, k, :], k == 0, k == 7, [Bwin, d_["Bh"]], [Bp])
                    act(z[:, m, :], ps[:, :], AF.Identity, [Bp, BzB], [Bz], bias=zB[:, m, q:q + 1])
                if half == 1:
                    S.dma("sp", US[q].rearrange("(k p) t -> p k t", p=128)[:, :, ti * 512:(ti + 1) * 512], z[:, 4:8, :], reads=[Bz])

            def a_main2(i):
                q, ti = tiles[i]
                d_ = stA.pop(i)
                z, Bz = d_["z"], d_["Bz"]
                t0 = ti * 512
                ab, Bab = abt.next()
                for j in range(4):
                    for half in range(2):
                        ps, Bp = PS.next()
                        for kc in range(4):
                            mm(ps[:, :], z[:, kc, j * 128:(j + 1) * 128], Gcat[:, kc, half * 512:(half + 1) * 512],
                               kc == 0, kc == 3, [Bz, BG], [Bp])
                        cp("dve" if half == 0 else "act", ab[:, j, half * 512:(half + 1) * 512], ps[:, :], [Bp], [Bab])
                for j in range(4):
                    S.dma("sp", AB[q][:, t0 + j * 128:t0 + (j + 1) * 128, :].rearrange("g p c -> p g c"),
                          ab[:, j, :].rearrange("p (g c) -> p g c", c=256), reads=[Bab])

            nA = len(tiles)
            if nA:
                a_load(0); a_square(0); a_norm(0)
                if nA > 1:
                    a_load(1)
            for i in range(nA):
                if i + 1 < nA:
                    a_square(i + 1)
                a_main1(i, 0)
                if i + 1 < nA:
                    a_norm(i + 1)
                a_main1(i, 1)
                if i + 2 < nA:
                    a_load(i + 2)
                a_main2(i)

        S.barrier()
        stG.close()
        S.dma("pool", wb_glu[:, :], w_glu_d[:, :], writes=[Bwb["w_glu"]])
        S.dma("pool", wb_out[:, :], w_out_d[:, :], writes=[Bwb["w_out"]])
        S.dma("pool", wb_mi[:, :], w_mi_d[:, :], writes=[Bwb["w_mi"]])
        S.dma("pool", wb_mo[:, :], w_mo_d[:, :], writes=[Bwb["w_mo"]])
        for q, L in enumerate(seqs):
            if 'F' in skip:
                break
            N1 = L // 128
            S.barrier()
            with ExitStack() as st:
                dc = dftc[q]
                Lo = louts[q]; nk2 = Lo // N1; kpb = min(512 // nk2, N1)
                FF1 = SB(st, "FF1", [N1, 2 * N1], BF16); FF2 = SB(st, "FF2", [N1, 2 * N1], BF16)
                Wc = SB(st, "Wc", [128, Lo], BF16); Ws = SB(st, "Ws", [128, Lo], BF16)
                Bc = Buf("dftc")
                S.dma("sp", FF1[:], dc["FF1"][:, :], writes=[Bc]); S.dma("sp", FF2[:], dc["FF2"][:, :], writes=[Bc])
                S.dma("sp", Wc[:], dc["Wc"][:, :], writes=[Bc]); S.dma("sp", Ws[:], dc["Ws"][:, :], writes=[Bc])
                X = SB(st, "X", [N1, 128, 256], BF16); BX = Buf("X")
                Ypr = Rot([(SB(st, "Yp%d" % i, [128, 128, 2, N1], BF16), Buf("Yp")) for i in range(1 if N1 == 128 else 2)])
                YFr = Rot([(SB(st, "YF%d" % i, [128, Lo], BF16), Buf("YF")) for i in range(1 if N1 == 128 else 2)])
                nb = 512 // (2 * N1)
                S.dma("sp", X[:], AB[q][0].rearrange("(a b) c -> a b c", b=128), reads=[], writes=[BX])
                for g4 in range(4):
                    Yp, BYp = Ypr.next(); YF, BYF = YFr.next()
                    for bi in range(128 // nb):
                        ps, Bp = PS.next()
                        for ci in range(nb):
                            c = bi * nb + ci
                            mm(ps[:, ci * 2 * N1:(ci + 1) * 2 * N1], X[:, :, c], FF1[:], True, False, [BX, Bc], [Bp])
                            mm(ps[:, ci * 2 * N1:(ci + 1) * 2 * N1], X[:, :, 128 + c], FF2[:], False, True, [BX, Bc], [Bp])
                        cp("act" if bi % 2 == 0 else "dve", Yp[:, bi * nb:(bi + 1) * nb, :, :],
                           ps[:, :].rearrange("p (c r k) -> p c r k", r=2, k=N1), [Bp], [BYp])
                    if g4 + 1 < 4:
                        S.dma("sp", X[:], AB[q][g4 + 1].rearrange("(a b) c -> a b c", b=128), reads=[], writes=[BX])
                    YFv = YF[:].rearrange("c (k2 k1) -> c k1 k2", k1=N1)
                    Wcv = Wc[:].rearrange("p (k2 k1) -> p k1 k2", k1=N1); Wsv = Ws[:].rearrange("p (k2 k1) -> p k1 k2", k1=N1)
                    for kb in range(N1 // kpb):
                        ps, Bp = PS.next()
                        for i in range(kpb):
                            k1 = kb * kpb + i
                            mm(ps[:, i * nk2:(i + 1) * nk2], Yp[:, :, 0, k1], Wcv[:, k1, :], True, False, [BYp, Bc], [Bp])
                            mm(ps[:, i * nk2:(i + 1) * nk2], Yp[:, :, 1, k1], Wsv[:, k1, :], False, True, [BYp, Bc], [Bp])
                        act(YFv[:, kb * kpb:(kb + 1) * kpb, :], ps[:, 0:kpb * nk2].rearrange("p (i k) -> p i k", k=nk2), AF.Identity,
                            [Bp, Bvecs], [BYF], bias=V("b_f")[:, g4:g4 + 1])
                    S.dma("sp", YFd[q][g4 * 128:(g4 + 1) * 128, :], YF[:], reads=[BYF])

        S.barrier()

        S.barrier()
        with ExitStack() as st:
            stbox[0] = st
            BS_ = [Buf("ssmS")]
            aS = SB(st, "aS", [128, 3, 32]); cS = SB(st, "cS", [128, 2, 32, 16]); bS = SB(st, "bS", [128, 2, 32, 16])
            S.dma("sp", aS[:], aS_d.rearrange("p a d c s -> p a (d c s)"), writes=BS_)
            S.dma("sp", cS[:], cS_d.rearrange("p a d c s h -> p a (d c s) h"), writes=BS_)
            S.dma("sp", bS[:], bS_d.rearrange("p a d c s h -> p a (d c s) h"), writes=BS_)
            PWS, crS, ciS, t1S, t2S = cx_setup("S", aS, 32, BS_)
            big = SB(st, "bigS", [128, 2, 32, 16])
            bb16 = lambda ap: ap.unsqueeze(2).to_broadcast([128, 32, 16])
            bbS = SB(st, "bbS", [128, 2, 32, 16])
            cmul(bbS[:, 0], bbS[:, 1], bb16(crS), bb16(ciS), bS[:, 0], bS[:, 1], big[:, 0], big[:, 1], BS_)
            caS = SB(st, "caS", [128, 9, 2, 32, 16])
            for e_ in range(9):
                cmul(caS[:, e_, 0], caS[:, e_, 1], bb16(PWS[:, e_, 0, :]), bb16(PWS[:, e_, 1, :]), cS[:, 0], cS[:, 1], big[:, 0], big[:, 1], BS_)
            OUT9 = SB(st, "OUT9", [128, 4, 2, 4, 9, 2, 32], BF16)
            Bm = SB(st, "Bm", [128, 32, 2, 32], BF16)
            for ri in range(2):
                ts("dve", Bm[:, :, ri, 0:16], bbS[:, ri], V("m0S"), None, ALU.mult, None, BS_ + [Bvecs], BS_)
                ts("dve", Bm[:, :, ri, 16:32], bbS[:, ri], V("m1S"), None, ALU.mult, None, BS_ + [Bvecs], BS_)
                for d in range(2):
                    for e_ in range(9):
                        ei = e_ if d == 0 else 8 - e_
                        src = caS[:, e_, ri, d * 16:(d + 1) * 16, :].rearrange("p (c s) h -> p c s h", s=4)
                        ts("dve", OUT9[:, :, d, :, ei, ri, 0:16], src, V("m0S") if ri == 0 else V("nm0S"), None, ALU.mult, None, BS_ + [Bvecs], BS_)
                        ts("dve", OUT9[:, :, d, :, ei, ri, 16:32], src, V("m1S") if ri == 0 else V("nm1S"), None, ALU.mult, None, BS_ + [Bvecs], BS_)
            for cc in range(4):
                S.dma("sp", OUTd[:, cc, 0].rearrange("p s t r k -> p s (t r k)"),
                      OUT9[:, cc, 0, :, 1:9].rearrange("p s t r k -> p s (t r k)"), reads=BS_)
                S.dma("sp", OUTd[:, cc, 1].rearrange("p s t r k -> p s (t r k)"),
                      OUT9[:, cc, 1, :, 0:8].rearrange("p s t r k -> p s (t r k)"), reads=BS_)
            a8r, a8i = PWS[:, 8, 0, :], PWS[:, 8, 1, :]
            tt("dve", t1S, a8r, a8r, ALU.mult, BS_, BS_); tt("dve", t2S, a8i, a8i, ALU.mult, BS_, BS_)
            tt("dve", t1S, t1S, t2S, ALU.add, BS_, BS_)
            act(r_sc[:], t1S, AF.Sqrt, BS_, [Bsc_c])
            S.op("dve", lambda e: e.reciprocal(out=t1S, in_=r_sc[:]), [Bsc_c] + BS_, BS_)
            TB = SB(st, "TB", [128, 32, 2, JB])
            Ep = SB(st, "Ep", [128, 2, 32]); Ep2 = SB(st, "Ep2", [128, 2, 32])
            tt("dve", Ep[:, 0, :], a8r, t1S, ALU.mult, BS_, BS_); tt("dve", Ep[:, 1, :], a8i, t1S, ALU.mult, BS_, BS_)
            cp("dve", E1[:], Ep[:], BS_, [Bsc_c])
            S.op("dve", lambda e: e.memset(TB[:, :, 0, 0:1], 1.0), BS_, BS_)
            S.op("dve", lambda e: e.memset(TB[:, :, 1, 0:1], 0.0), BS_, BS_)
            tbt = SB(st, "tbt", [128, 2, 32, JB // 2])
            mlen = 1
            while mlen < JB:
                eb = lambda ap: ap.unsqueeze(2).to_broadcast([128, 32, mlen])
                cmul(TB[:, :, 0, mlen:2 * mlen], TB[:, :, 1, mlen:2 * mlen], TB[:, :, 0, 0:mlen], TB[:, :, 1, 0:mlen],
                     eb(Ep[:, 0, :]), eb(Ep[:, 1, :]), tbt[:, 0, :, 0:mlen], tbt[:, 1, :, 0:mlen], BS_)
                tt("dve", Ep2[:, 0, :], Ep[:, 0, :], Ep[:, 0, :], ALU.mult, BS_, BS_); tt("dve", Ep2[:, 1, :], Ep[:, 1, :], Ep[:, 1, :], ALU.mult, BS_, BS_)
                tt("dve", Ep[:, 1, :], Ep[:, 0, :], Ep[:, 1, :], ALU.mult, BS_, BS_); ts("dve", Ep[:, 1, :], Ep[:, 1, :], 2.0, None, ALU.mult, None, BS_, BS_)
                tt("dve", Ep[:, 0, :], Ep2[:, 0, :], Ep2[:, 1, :], ALU.subtract, BS_, BS_)
                mlen *= 2
            Ep3 = SB(st, "Ep3", [128, 2, 32])
            cp("dve", Ep2[:], Ep[:], BS_, BS_)
            tt("dve", Ep3[:, 0, :], Ep[:, 0, :], Ep[:, 0, :], ALU.mult, BS_, BS_); tt("dve", t2S, Ep[:, 1, :], Ep[:, 1, :], ALU.mult, BS_, BS_)
            tt("dve", Ep3[:, 0, :], Ep3[:, 0, :], t2S, ALU.subtract, BS_, BS_)
            tt("dve", Ep3[:, 1, :], Ep[:, 0, :], Ep[:, 1, :], ALU.mult, BS_, BS_); ts("dve", Ep3[:, 1, :], Ep3[:, 1, :], 2.0, None, ALU.mult, None, BS_, BS_)
            tt("dve", EJ[:, 0, :], Ep3[:, 0, :], Ep3[:, 0, :], ALU.mult, BS_, [Bsc_c]); tt("dve", t2S, Ep3[:, 1, :], Ep3[:, 1, :], ALU.mult, BS_, BS_)
            tt("dve", EJ[:, 0, :], EJ[:, 0, :], t2S, ALU.subtract, BS_ + [Bsc_c], [Bsc_c])
            tt("dve", EJ[:, 1, :], Ep3[:, 0, :], Ep3[:, 1, :], ALU.mult, BS_, [Bsc_c]); ts("dve", EJ[:, 1, :], EJ[:, 1, :], 2.0, None, ALU.mult, None, [Bsc_c], [Bsc_c])
            TBx = SB(st, "TBx", [128, 4, 2, JB2]); tbx = SB(st, "tbx_t", [128, 2, 4, JB2 // 2])
            for d in range(2):
                for cc in range(4):
                    f0 = d * 16 + cc * 4
                    cp("dve", TBx[:, :, :, 0:JB], TB[:, f0:f0 + 4], BS_, BS_)
                    for (m0, Epw) in ((JB, Ep2), (2 * JB, Ep3)):
                        eb = lambda ap, m0=m0: ap.unsqueeze(2).to_broadcast([128, 4, m0])
                        cmul(TBx[:, :, 0, m0:2 * m0], TBx[:, :, 1, m0:2 * m0], TBx[:, :, 0, 0:m0], TBx[:, :, 1, 0:m0],
                             eb(Epw[:, 0, f0:f0 + 4]), eb(Epw[:, 1, f0:f0 + 4]), tbx[:, 0, :, 0:m0], tbx[:, 1, :, 0:m0], BS_)
                    S.dma("pool", TBd[:, cc, d * 4:(d + 1) * 4].rearrange("p s r j -> p (s r j)"),
                          TBx[:].rearrange("p s r j -> p (s r j)"), reads=BS_, writes=BS_)
            zer = SB(st, "zer", [128, 128], BF16); Bzer = Buf()
            S.op("pool", lambda e: e.memset(zer[:], 0.0), [], [Bzer])
            KM = SB(st, "KM", [128, 4, 15, 128], BF16); BKM = Buf()
            ktmp = SB(st, "ktmp", [128, 128]); Bkt = Buf()
            for cc in range(4):
                for dl in range(-7, 8):
                    ps, Bp = PS.next()
                    mm(ps[:, 0:128], zer[:], zer[:], True, False, [Bzer], [Bp])
                    dirs = ([0] if dl >= 0 else []) + ([1] if dl <= 0 else [])
                    n_mm = len(dirs) * 8; i_mm = 0
                    for d in dirs:
                        for pi in range(4):
                            Fi = d * 16 + cc * 4 + pi
                            for ri in range(2):
                                i_mm += 1
                                mm(ps[32 * pi:32 * pi + 32, 32 * pi:32 * pi + 32], Bm[:, Fi, ri, :], OUT9[:, cc, d, pi, abs(dl) if d == 0 else 8 - abs(dl), ri, :],
                                   False, i_mm == n_mm, BS_ + [Bzer], [Bp], tile_position=(0, 32 * pi))
                    if dl != 0:
                        cp("act", KM[:, cc, dl + 7, :], ps[:, 0:128], [Bp], [BKM])
                    else:
                        cp("act", ktmp[:], ps[:, 0:128], [Bp], [Bkt])
                        stt("dve", KM[:, cc, 7, :], ident[:], V("ssm_d")[:, cc:cc + 1], ktmp[:], ALU.mult, ALU.add, [Bident, Bvecs, Bkt], [BKM])
            S.dma("sp", KMd.rearrange("p c t k -> p (c t k)"), KM[:].rearrange("p c t k -> p (c t k)"), reads=[BKM])
        S.barrier()

        def bs_consts(st, cc):
            INm = SB(st, "INm", [128, 2, 8, 2, 128], BF16); OUTm = SB(st, "OUTm", [128, 2, 4, 8, 2, 32], BF16)
            KMt = SB(st, "KMt", [128, 15, 128], BF16); Bcst = Buf("bs_const")
            S.dma("sp", INm[:].rearrange("p d s r k -> p (d s r k)"), INd[:, cc].rearrange("p d s r k -> p (d s r k)"), writes=[Bcst])
            S.dma("sp", OUTm[:].rearrange("p d s t r k -> p (d s t r k)"), OUTd[:, cc].rearrange("p d s t r k -> p (d s t r k)"), writes=[Bcst])
            S.dma("sp", KMt[:].rearrange("p t k -> p (t k)"), KMd[:, cc].rearrange("p t k -> p (t k)"), writes=[Bcst])
            TBb = SB(st, "TBb", [128, 8, 2, JB2], BF16); BTB = Buf()
            S.dma("sp", TBb[:].rearrange("p s r j -> p (s r j)"), TBd[:, cc].rearrange("p s r j -> p (s r j)"), writes=[BTB])
            return INm, OUTm, KMt, Bcst, TBb, BTB

        def bs_group(qs, cc, consts, urs, after_deint):
            INm, OUTm, KMt, Bcst, TBb, BTB = consts
            S.barrier()
            with ExitStack() as st:
                NBUF = 4
                Sbr = Rot([(SB(st, "Sb%d" % i, [128, 2, 512], BF16), Buf()) for i in range(NBUF)])
                twr = Rot([(SB(st, "tw%d" % i, [128, 2, 2, 512], BF16), Buf()) for i in range(NBUF)])
                Gtr = Rot([(SB(st, "Gt%d" % i, [128, 2, 512], BF16), Buf()) for i in range(NBUF)])
                Gsr = Rot([(SB(st, "Gs%d" % i, [128, 2, 512], BF16), Buf()) for i in range(NBUF)])
                inir = Rot([(SB(st, "ini%d" % i, [128, 4]), Buf()) for i in range(NBUF)])
                sgn = SB(st, "sgn", [128, 2]); Bsgn = Buf()
                S.op("pool", lambda e: e.memset(sgn[:, 0:1], -1.0), [], [Bsgn])
                S.op("pool", lambda e: e.memset(sgn[:, 1:2], 1.0), [], [Bsgn])
                sgn2 = SB(st, "sgn2", [128, 2])
                S.op("pool", lambda e: e.memset(sgn2[:, 0:1], 1.0), [], [Bsgn])
                S.op("pool", lambda e: e.memset(sgn2[:, 1:2], -1.0), [], [Bsgn])
                ctxs = []
                for q in qs:
                    L = seqs[q]; NC = L // 8; Lo = louts[q]; rot = Lo < L
                    c = dict(q=q, L=L, NC=NC, nblk=NC // 512, Lo=Lo, rot=rot, nco=min(512, Lo // 8))
                    c["nblk_o"] = (Lo // 8) // c["nco"]
                    Ur, BUr = urs[q]
                    U = SB(st, "U8", [128, 8, NC], BF16); BU = Buf("U8")
                    Urv = Ur[:].rearrange("p (j s) -> p s j", s=8)
                    cp("act", U[:, 0:4, :], Urv[:, 0:4, :], [BUr], [BU])
                    cp("dve", U[:, 4:8, :], Urv[:, 4:8, :], [BUr], [BU])
                    after_deint(q)
                    NCs = 512 if rot else NC
                    Hb = SB(st, "Hb", [128, 8, 2, NCs + 2], BF16); BHb = [Buf("Hb%d" % i) for i in range(8)]
                    S.op("pool", lambda e, Hb=Hb: e.memset(Hb[:, :, :, 0:1], 0.0), [], BHb)
                    S.op("pool", lambda e, Hb=Hb, NCs=NCs: e.memset(Hb[:, :, :, NCs + 1:NCs + 2], 0.0), [], BHb)
                    if rot:
                        Mskb = SB(st, "Mskb", [128, 2, NC], BF16); BMsk = Buf()
                        S.dma("sp", Mskb[:], mskd[q][:, :, :], writes=[BMsk])
                        c.update(Mskb=Mskb, BMsk=BMsk)
                    Yst = Rot([(SB(st, "Yst%d" % i, [128, c["nco"] * 8], BF16), Buf()) for i in range(2)])
                    c.update(U=U, BU=BU, Hb=Hb, BHb=BHb, Yst=Yst)
                    ctxs.append(c)

                def dir_gen(c, d):
                    rot, nblk, U, BU, Hb, BHb = c["rot"], c["nblk"], c["U"], c["BU"], c["Hb"], c["BHb"]
                    carry = [None] * 4
                    blocks = [(b_, rot) for b_ in range(nblk)] if d == 0 else [(b_, rot) for b_ in range(nblk - 1, -1, -1)]
                    if rot and d == 0:
                        blocks.append((0, False))
                    for bidx, (b, masked) in enumerate(blocks):
                        stored = (not rot) or (b == 0 and (d == 1 or bidx == nblk))
                        banks = [(PS.next(), PS.next()) for _ in range(4)]
                        for ri in range(2):
                            for s_ in range(8):
                                for pi in range(4):
                                    ps, Bp = banks[pi][ri]
                                    mm(ps[:, :], INm[32 * pi:32 * pi + 32, d, s_, ri, :], U[32 * pi:32 * pi + 32, s_, b * 512:(b + 1) * 512],
                                       s_ == 0, s_ == 7, [Bcst, BU], [Bp], tile_position=(32 * pi, 0))
                        T = []
                        for pi in range(4):
                            Sb, BSb = Sbr.next(); tw, Btw = twr.next(); Gt, BGt = Gtr.next(); Gs, BGs = Gsr.next()
                            (psr, Bpr), (psi, Bpi) = banks[pi]
                            cp("act", Sb[:, 0, :], psr[:, :], [Bpr], [BSb])
                            cp("act", Sb[:, 1, :], psi[:, :], [Bpi], [BSb])
                            tl = d * 4 + pi
                            tb = TBb[:, tl, :, :] if d == 0 else TBb[:, tl, :, ::-1]
                            T.append(dict(Sb=Sb, BSb=BSb, tw=tw, Btw=Btw, Gt=Gt, BGt=BGt, Gs=Gs, BGs=BGs, tl=tl,
                                          Fi=d * 16 + cc * 4 + pi, tb4=tb.unsqueeze(1).to_broadcast([128, 2, 2, 512])))
                        yield
                        for t in T:
                            tt("dve", t["tw"][:], t["Sb"][:].unsqueeze(2).to_broadcast([128, 2, 2, 512]), t["tb4"], ALU.mult, [t["BSb"], BTB], [t["Btw"]])
                        yield
                        for t in T:
                            tw, Gt = t["tw"], t["Gt"]
                            tt("dve", Gt[:, 0, :], tw[:, 0, 0, :], tw[:, 1, 1, :], ALU.add, [t["Btw"]], [t["BGt"]])
                            tt("dve", Gt[:, 1, :], tw[:, 1, 0, :], tw[:, 0, 1, :], ALU.subtract, [t["Btw"]], [t["BGt"]])
                            if masked:
                                tt("dve", Gt[:], Gt[:], c["Mskb"][:, d, b * 512:(b + 1) * 512].unsqueeze(1).to_broadcast([128, 2, 512]), ALU.mult,
                                   [t["BGt"], c["BMsk"]], [t["BGt"]])
                        yield
                        for pi, t in enumerate(T):
                            Fi, tl = t["Fi"], t["tl"]
                            if carry[pi] is None:
                                t["iv"] = (0.0, 0.0); t["rd"] = [t["BGt"], Bsc_c]
                            else:
                                c_fwd, c_swp, BGp = carry[pi]
                                ini, Bini = inir.next()
                                stt("dve", ini[:, 2:4], c_swp, EJ[:, 1, Fi:Fi + 1], sgn[:], ALU.mult, ALU.mult, [BGp, Bsc_c, Bsgn], [Bini])
                                stt("dve", ini[:, 0:2], c_fwd, EJ[:, 0, Fi:Fi + 1], ini[:, 2:4], ALU.mult, ALU.add, [BGp, Bsc_c, Bini], [Bini])
                                t["iv"] = (ini[:, 0:1], ini[:, 1:2]); t["rd"] = [t["BGt"], Bsc_c, Bini]
                                if rot and d == 0 and bidx == nblk:
                                    stt("dve", ini[:, 2:4], ini[:, 0:2][:, ::-1], E1[:, 1, Fi:Fi + 1], sgn2[:], ALU.mult, ALU.mult,
                                        [Bini, Bsc_c, Bsgn], [Bini])
                                    stt("dve", Hb[:, tl, :, 0], ini[:, 0:2], E1[:, 0, Fi:Fi + 1], ini[:, 2:4], ALU.mult, ALU.add,
                                        [Bini, Bsc_c], [Bini, BHb[tl]])
                        yield
                        for pi, t in enumerate(T):
                            Gs, Gt = t["Gs"], t["Gt"]
                            rdec = r_sc[:, t["Fi"]:t["Fi"] + 1].to_broadcast([128, 512])
                            for ri in range(2):
                                o_ = Gs[:, ri, :] if d == 0 else Gs[:, ri, ::-1]
                                i_ = Gt[:, ri, :] if d == 0 else Gt[:, ri, ::-1]
                                iv = t["iv"][ri]
                                S.op("dve", lambda e, o_=o_, i_=i_, iv=iv, rdec=rdec: e.tensor_tensor_scan(out=o_, data0=rdec, data1=i_, initial=iv,
                                                                                                      op0=ALU.mult, op1=ALU.add), t["rd"], [t["BGs"]])
                            last = 511 if d == 0 else 0
                            carry[pi] = (Gs[:, :, last], Gs[:, ::-1, last], t["BGs"])
                        yield
                        if not stored:
                            continue
                        for t in T:
                            tt("dve", t["tw"][:], t["Gs"][:].unsqueeze(2).to_broadcast([128, 2, 2, 512]), t["tb4"], ALU.mult, [t["BGs"], BTB], [t["Btw"]])
                        yield
                        c0 = 1 + b * 512
                        for t in T:
                            tw, tl = t["tw"], t["tl"]
                            tt("dve", Hb[:, tl, 0, c0:c0 + 512], tw[:, 0, 0, :], tw[:, 1, 1, :], ALU.subtract, [t["Btw"]], [BHb[tl]])
                            tt("dve", Hb[:, tl, 1, c0:c0 + 512], tw[:, 1, 0, :], tw[:, 0, 1, :], ALU.add, [t["Btw"]], [BHb[tl]])
                        yield

                def out_gen(c):
                    q, nco, U, BU, Hb, BHb = c["q"], c["nco"], c["U"], c["BU"], c["Hb"], c["BHb"]
                    for b in range(c["nblk_o"]):
                        ys, Bys = c["Yst"].next()
                        ysv = ys[:].rearrange("p (j t) -> p t j", t=8)
                        for t_ in range(8):
                            ps, Bp = PS.next()
                            for s_ in range(8):
                                mm(ps[:, 0:nco], KMt[:, t_ - s_ + 7, :], U[:, s_, b * nco:(b + 1) * nco], s_ == 0, False, [Bcst, BU], [Bp])
                            cnt = 0
                            for d in range(2):
                                c0 = b * nco + (0 if d == 0 else 2)
                                for pi in range(4):
                                    tl = d * 4 + pi
                                    for ri in range(2):
                                        cnt += 1
                                        mm(ps[32 * pi:32 * pi + 32, 0:nco], OUTm[:, d, pi, t_, ri, :], Hb[:, tl, ri, c0:c0 + nco],
                                           False, cnt == 16, [Bcst, BHb[tl]], [Bp], tile_position=(0, 32 * pi))
                            act(ysv[:, t_, :], ps[:, 0:nco], AF.Gelu_apprx_tanh, [Bp], [Bys])
                            yield
                        S.dma("sp", GS[q][cc * 128:(cc + 1) * 128, b * nco * 8:(b + 1) * nco * 8], ys[:], reads=[Bys])

                def step(g):
                    try:
                        next(g)
                        return True
                    except StopIteration:
                        return False

                pending = None
                for c in ctxs:
                    for d in range(2):
                        g_ = dir_gen(c, d)
                        while step(g_):
                            if pending is not None and not step(pending):
                                pending = None
                    while pending is not None:
                        if not step(pending):
                            pending = None
                    pending = out_gen(c)
                while pending is not None:
                    if not step(pending):
                        pending = None

        if 'S' not in skip:
            rot_qs = [q for q in range(nseq) if louts[q] < seqs[q]]
            oth_qs = [q for q in range(nseq) if louts[q] == seqs[q]]
            S.barrier()
            with ExitStack() as stU:
                urs = {}
                for q in rot_qs:
                    urs[q] = (SB(stU, "UrP%d" % q, [128, seqs[q]], BF16), Buf("UrP"))
                    S.dma("sp", urs[q][0][:], US[q][0:128, :], writes=[urs[q][1]])
                for cc in range(4):
                    S.barrier()
                    with ExitStack() as stc:
                        consts = bs_consts(stc, cc)
                        for q in oth_qs:
                            urs[q] = (SB(stc, "UrS%d" % q, [128, seqs[q]], BF16), Buf("UrS"))
                            S.dma("sp", urs[q][0][:], US[q][cc * 128:(cc + 1) * 128, :], writes=[urs[q][1]])

                        def after_deint(q, cc=cc):
                            if q in rot_qs and cc + 1 < 4:
                                S.dma("sp", urs[q][0][:], US[q][(cc + 1) * 128:(cc + 2) * 128, :], writes=[urs[q][1]])

                        for q in rot_qs:
                            bs_group([q], cc, consts, urs, after_deint)
                        for i in range(0, len(oth_qs), 2):
                            bs_group(oth_qs[i:i + 2], cc, consts, urs, after_deint)
                        S.barrier()
        S.barrier()
        with ExitStack() as st:
            wglu = SB(st, "wglu", [128, 4, 1024], BF16); wout = SB(st, "wout", [128, 8, 1024], BF16); Bw1 = Buf("w_c1")
            S.dma("sp", wglu[:], wb_glu.rearrange("(k p) n -> p k n", p=128), reads=[Bwb["w_glu"]], writes=[Bw1])
            S.dma("sp", wout[:], wb_out.rearrange("(k p) n -> p k n", p=128), reads=[Bwb["w_out"]], writes=[Bw1])
            xin = Rot([(SB(st, "c1x%d" % i, [128, 8, 512]), Buf()) for i in range(2)])
            yfin = Rot([(SB(st, "c1yf%d" % i, [128, 4, 512], BF16), Buf()) for i in range(2)])
            gsin = Rot([(SB(st, "c1gs%d" % i, [128, 4, 512], BF16), Buf()) for i in range(2)])
            valr = Rot([(SB(st, "val%d" % i, [128, 4, 512]), Buf()) for i in range(2)])
            sgr = Rot([(SB(st, "sg%d" % i, [128, 4, 512]), Buf()) for i in range(2)])
            sqbr = Rot([(SB(st, "sqb%d" % i, [128, 8, 512], BF16), Buf()) for i in range(2)])
            rsbr = Rot([(SB(st, "rsb%d" % i, [128, 2, 512]), Buf()) for i in range(2)])
            mgr = Rot([(SB(st, "mg%d" % i, [128, 8, 512], BF16), Buf()) for i in range(2)])
            x1o = Rot([(SB(st, "x1o%d" % i, [128, 8, 512]), Buf()) for i in range(2)])
            tiles1 = [] if 'C' in skip else [(q, ti) for q, L in enumerate(seqs) for ti in range(louts[q] // 512)]
            st1 = {}

            def c_load(i):
                q, ti = tiles1[i]
                t0 = ti * 512
                xt, Bx = xin.next(); yf, Byf = yfin.next(); gs, Bgs = gsin.next()
                S.dma("sp", yf[:], YFd[q].rearrange("(k p) t -> p k t", p=128)[:, :, t0:t0 + 512], writes=[Byf])
                S.dma("sp", gs[:], GS[q].rearrange("(k p) t -> p k t", p=128)[:, :, t0:t0 + 512], writes=[Bgs])
                S.dma("sp", xt[:], xT[q].rearrange("(k p) t -> p k t", p=128)[:, :, t0:t0 + 512], writes=[Bx])
                st1[i] = dict(xt=xt, Bx=Bx, yf=yf, Byf=Byf, gs=gs, Bgs=Bgs)

            def c_p1(i):
                d_ = st1[i]
                val, Bval = valr.next(); sg, Bsg = sgr.next(); sqb, Bsqb = sqbr.next()
                act(sqb[:, 0:4, :], d_["yf"][:], AF.Square, [d_["Byf"]], [Bsqb])
                for m in range(8):
                    ps, Bp = PS.next()
                    for k in range(4):
                        mm(ps[:, :], wglu[:, k, m * 128:(m + 1) * 128], d_["gs"][:, k, :], k == 0, k == 3, [Bw1, d_["Bgs"]], [Bp])
                    if m < 4:
                        act(val[:, m, :], ps[:, :], AF.Identity, [Bp, Bvecs], [Bval], bias=V("b_glu")[:, m:m + 1])
                    else:
                        act(sg[:, m - 4, :], ps[:, :], AF.Sigmoid, [Bp, Bvecs], [Bsg], bias=V("b_glu")[:, m:m + 1])
                tt("dve", val[:], val[:], sg[:], ALU.mult, [Bval, Bsg], [Bval])
                act(sqb[:, 4:8, :], val[:], AF.Square, [Bval], [Bsqb])
                d_.update(val=val, Bval=Bval, sqb=sqb, Bsqb=Bsqb)

            def c_p2(i):
                d_ = st1[i]
                rsb, Brsb = rsbr.next(); mg, Bmg = mgr.next()
                for br in range(2):
                    ps, Bp = PS.next()
                    for k in range(4):
                        mm(ps[:, :], ones_bf[:], d_["sqb"][:, br * 4 + k, :], k == 0, k == 3, [Bones, d_["Bsqb"]], [Bp])
                    act(rsb[:, br, :], ps[:, :], AF.Sqrt, [Bp], [Brsb], bias=EPS, scale=1.0 / 512)
                S.op("dve", lambda e: e.reciprocal(out=rsb[:], in_=rsb[:]), [Brsb], [Brsb])
                for k in range(4):
                    stt("dve", mg[:, k, :], d_["yf"][:, k, :], V("g_f")[:, k:k + 1], rsb[:, 0, :], ALU.mult, ALU.mult, [d_["Byf"], Bvecs, Brsb], [Bmg])
                    stt("dve", mg[:, 4 + k, :], d_["val"][:, k, :], V("g_s")[:, k:k + 1], rsb[:, 1, :], ALU.mult, ALU.mult, [d_["Bval"], Bvecs, Brsb], [Bmg])
                d_.update(mg=mg, Bmg=Bmg)

            def c_p3(i):
                q, ti = tiles1[i]
                d_ = st1.pop(i)
                xo, Bxo = x1o.next()
                for m in range(8):
                    ps, Bp = PS.next()
                    for k in range(8):
                        mm(ps[:, :], wout[:, k, m * 128:(m + 1) * 128], d_["mg"][:, k, :], k == 0, k == 7, [Bw1, d_["Bmg"]], [Bp])
                    stt("dve", xo[:, m, :], ps[:, :], mod[:, 16 + m, q:q + 1], d_["xt"][:, m, :], ALU.mult, ALU.add, [Bp, Bmod, d_["Bx"]], [Bxo])
                S.dma("sp", X1[q].rearrange("(k p) t -> p k t", p=128)[:, :, ti * 512:(ti + 1) * 512], xo[:], reads=[Bxo])

            n1 = len(tiles1)
            if n1:
                c_load(0)
                if n1 > 1:
                    c_load(1)
                c_p1(0); c_p2(0)
            for i in range(n1):
                if i + 1 < n1:
                    c_p1(i + 1)
                c_p3(i)
                if i + 1 < n1:
                    c_p2(i + 1)
                if i + 2 < n1:
                    c_load(i + 2)
        S.barrier()
        NT = 256
        with ExitStack() as st:
            wmi = SB(st, "wmi", [128, 8, 4096], BF16); wmo = SB(st, "wmo", [128, 32, 1024], BF16); Bw2 = Buf("w_c2")
            Bw2i = Buf("wmi"); Bw2o = Buf("wmo"); Bw2 = Bw2i
            S.dma("sp", wmi[:], wb_mi.rearrange("(k p) n -> p k n", p=128), reads=[Bwb["w_mi"]], writes=[Bw2i])
            for i in range(4):
                S.dma("sp", wmo[:, i * 8:(i + 1) * 8, :], wb_mo[i * 1024:(i + 1) * 1024, :].rearrange("(k p) n -> p k n", p=128), reads=[Bwb["w_mo"]], writes=[Bw2o])
            for m in range(32):
                ps, Bp = PS.next()
                for k in range(8):
                    mm(ps[:, 0:nseq], wmi[:, k, m * 128:(m + 1) * 128], sh[:, 8 + k, :], k == 0, k == 7, [Bw2i, Bsh], [Bp])
                act(mB[:, m, :], ps[:, 0:nseq], AF.Identity, [Bp, Bvecs], [BmB], bias=V("b_mlp_in")[:, m:m + 1])
            x1in = Rot([(SB(st, "c2x%d" % i, [128, 8, NT]), Buf()) for i in range(2)])
            x2o = Rot([(SB(st, "c2y%d" % i, [128, 8, NT]), Buf()) for i in range(2)])
            sq2r = Rot([(SB(st, "sq2_%d" % i, [128, 8, NT], BF16), Buf()) for i in range(2)])
            rs2r = Rot([(SB(st, "rs2_%d" % i, [128, NT]), Buf()) for i in range(2)])
            h2r = Rot([(SB(st, "h2_%d" % i, [128, 8, NT], BF16), Buf()) for i in range(2)])
            aa = SB(st, "aa", [128, 32, NT], BF16); Baa = [Buf() for _ in range(32)]
            rl = Rot([(SB(st, "rl%d" % i, [128, NT]), Buf()) for i in range(3)])
            tiles2 = [] if 'D' in skip else [(q, ti) for q, L in enumerate(seqs) for ti in range(louts[q] // NT)]
            st2 = {}

            def d_load(i):
                q, ti = tiles2[i]
                x1, Bx1 = x1in.next()
                S.dma("sp", x1[:], X1[q].rearrange("(k p) t -> p k t", p=128)[:, :, ti * NT:(ti + 1) * NT], writes=[Bx1])
                st2[i] = dict(x1=x1, Bx1=Bx1)

            def d_square(i):
                d_ = st2[i]
                sq, Bsq = sq2r.next()
                act(sq[:], d_["x1"][:], AF.Square, [d_["Bx1"]], [Bsq])
                d_.update(sq=sq, Bsq=Bsq)

            def d_norm(i):
                q, ti = tiles2[i]
                d_ = st2[i]
                rs, Brs = rs2r.next(); h2, Bh2 = h2r.next()
                ps, Bp = PS.next()
                for k in range(8):
                    mm(ps[:, 0:NT], ones_bf[:], d_["sq"][:, k, :], k == 0, k == 7, [Bones, d_["Bsq"]], [Bp])
                act(rs[:], ps[:, 0:NT], AF.Sqrt, [Bp], [Brs], bias=EPS, scale=1.0 / 1024)
                S.op("dve", lambda e: e.reciprocal(out=rs[:], in_=rs[:]), [Brs], [Brs])
                x1, Bx1 = d_["x1"], d_["Bx1"]
                for k in range(8):
                    stt("dve", h2[:, k, :], x1[:, k, :], A2[:, k, q:q + 1], rs[:], ALU.mult, ALU.mult, [Bx1, BA2, Brs], [Bh2])
                for m in range(8):
                    act(x1[:, m, :], x1[:, m, :], AF.Identity, [Bx1, Bbg2, Bh2], [Bx1], bias=bg2[:, m, q:q + 1])
                d_.update(h2=h2, Bh2=Bh2)

            def d_mlp_in(i):
                q, ti = tiles2[i]
                d_ = st2[i]
                for f_ in range(32):
                    ps, Bp = PS.next()
                    for k in range(8):
                        mm(ps[:, 0:NT], wmi[:, k, f_ * 128:(f_ + 1) * 128], d_["h2"][:, k, :], k == 0, k == 7, [Bw2, d_["Bh2"]], [Bp])
                    rt, Brt = rl.next()
                    act(rt[:], ps[:, 0:NT], AF.Relu, [Bp, BmB], [Brt], bias=mB[:, f_, q:q + 1])
                    tt("dve" if f_ % 4 == 3 else "pool", aa[:, f_, :], rt[:], rt[:], ALU.mult, [Brt], [Baa[f_]])

            def d_mlp_out(i):
                q, ti = tiles2[i]
                d_ = st2[i]
                x2, Bx2 = x2o.next()
                for m in range(8):
                    ps, Bp = PS.next()
                    for f_ in range(32):
                        mm(ps[:, 0:NT], wmo[:, f_, m * 128:(m + 1) * 128], aa[:, f_, :], f_ == 0, f_ == 31, [Bw2o, Baa[f_]], [Bp])
                    stt("dve", x2[:, m, :], ps[:, 0:NT], mod[:, 40 + m, q:q + 1], d_["x1"][:, m, :], ALU.mult, ALU.add, [Bp, Bmod, d_["Bx1"]], [Bx2])
                d_.update(x2=x2, Bx2=Bx2)

            def d_tail_sq(i):
                d_ = st2[i]
                sq, Bsq = sq2r.next()
                act(sq[:], d_["x2"][:], AF.Square, [d_["Bx2"]], [Bsq])
                d_.update(sqf=sq, Bsqf=Bsq)

            def d_tail(i):
                q, ti = tiles2[i]
                d_ = st2.pop(i)
                x2, Bx2 = d_["x2"], d_["Bx2"]
                rs, Brs = rs2r.next()
                ps, Bp = PS.next()
                for k in range(8):
                    mm(ps[:, 0:NT], ones_bf[:], d_["sqf"][:, k, :], k == 0, k == 7, [Bones, d_["Bsqf"]], [Bp])
                act(rs[:], ps[:, 0:NT], AF.Sqrt, [Bp], [Brs], bias=EPS, scale=1.0 / 1024)
                S.op("dve", lambda e: e.reciprocal(out=rs[:], in_=rs[:]), [Brs], [Brs])
                for m in range(8):
                    stt("dve", x2[:, m, :], x2[:, m, :], V("g_final")[:, m:m + 1], rs[:], ALU.mult, ALU.mult, [Bx2, Bvecs, Brs], [Bx2])
                S.dma("sp", yT[q].rearrange("(k p) t -> p k t", p=128)[:, :, ti * NT:(ti + 1) * NT], x2[:], reads=[Bx2])

            n2 = len(tiles2)
            if n2:
                d_load(0); d_square(0); d_norm(0)
            for i in range(n2):
                if i + 1 < n2:
                    d_load(i + 1)
                if i > 0:
                    d_tail_sq(i - 1)
                if i + 1 < n2:
                    d_square(i + 1)
                d_mlp_in(i)
                if i + 1 < n2:
                    d_norm(i + 1)
                if i > 0:
                    d_tail(i - 1)
                d_mlp_out(i)
            if n2:
                d_tail_sq(n2 - 1); d_tail(n2 - 1)
        S.finish()
    print("n_instr", S.n_instr, {k: v.count for k, v in S.E.items()})
    return nc


def make_inmap(inp, com, vecs, lay, xs, cs):
    nseq = len(xs)
    v = vecs.copy()
    o, w = lay["cT"]
    cT = np.stack([_pk(c) for c in cs], axis=2)
    v[:, o:o + w] = cT.reshape(128, 8 * nseq)
    m = dict(com)
    m["vecs"] = v
    for q, x in enumerate(xs):
        m["xT%d" % q] = np.ascontiguousarray(x.T)
        L = x.shape[0]
        for k, a in _dft_consts(L).items():
            m["%s_q%d" % (k, q)] = a
    return m


def prompt_inputs(xp, i, L=16384, Lout=2048):
    s = Lout * i
    d = {"xT0": np.ascontiguousarray(np.roll(xp, -s, axis=0).T)}
    for k, a in _dft_consts(L, rot_i=i, Lout=Lout).items():
        d["%s_q0" % k] = a
    NC = L // 8
    j0 = (L - s) // 8
    j = np.arange(NC)
    msk = np.stack([(j >= j0), (j < j0)]).astype(np.float32)
    d["msk_q0"] = np.ascontiguousarray(np.broadcast_to(msk[None], (128, 2, NC))).astype(ml_dtypes.bfloat16)
    return d


SEQS = [(16384, 2048), (4096, 4096), (4096, 4096)]
_NC_CACHE = {}


def kernel(**inputs):
    inp = {k: np.asarray(v) for k, v in inputs.items()}
    nseq = len(SEQS)
    com, vecs, lay = _host_common(inp, nseq)
    if "nc" not in _NC_CACHE:
        _NC_CACHE["nc"] = build(SEQS)
    nc = _NC_CACHE["nc"]
    consts = {}
    for k, a in _dft_consts(4096).items():
        consts["%s_q1" % k] = a
        consts["%s_q2" % k] = a
    o, w = lay["cT"]
    in_maps = []
    for i in range(8):
        cs = [inp["c_prompt"][0], inp["c_sample"][2 * i], inp["c_sample"][2 * i + 1]]
        v = vecs.copy()
        v[:, o:o + w] = np.stack([_pk(c) for c in cs], axis=2).reshape(128, 8 * nseq)
        m = dict(com)
        m.update(consts)
        m["vecs"] = v
        m.update(prompt_inputs(inp["x_prompt"][0], i))
        m["xT1"] = np.ascontiguousarray(inp["x_sample"][2 * i].T)
        m["xT2"] = np.ascontiguousarray(inp["x_sample"][2 * i + 1].T)
        in_maps.append(m)
    res = run_bass_kernel_spmd(nc, in_maps, core_ids=list(range(8)))
    R = res.results
    y_prompt = np.concatenate([np.asarray(R[i]["yT0"]).T for i in range(8)], axis=0)[None].astype(np.float32)
    y_sample = np.stack([np.ascontiguousarray(np.asarray(R[i]["yT%d" % (1 + j)]).T) for i in range(8) for j in range(2)]).astype(np.float32)
    return (y_prompt, y_sample)
```
